# Optimizing a Trainium2 kernel written in Bass

```python
import jax, jax.numpy as jnp
from jax import lax
import numpy as np

D_MODEL = 2048
BATCH = 1
SEQ = 16384
DEPTH = 1

CHUNK = 64
PLE_DIM = 256
EPS = 1e-6

ATT_HEAD_DIM = 64
ATT_WIDTH = D_MODEL // 2
ATT_HEADS = ATT_WIDTH // ATT_HEAD_DIM
BAND_CHUNKS = 9
MAX_REL_DIST = 256

GLA_HEADS = 4
GLA_WIDTH = D_MODEL // 2
GLA_DV = GLA_WIDTH // GLA_HEADS
GLA_KEY_WIDTH = GLA_WIDTH // 2
GLA_DK = GLA_KEY_WIDTH // GLA_HEADS
GLA_GATE_RANK = 16
GLA_GATE_TAU = 16.0

MIX_WIDTH = ATT_WIDTH + GLA_WIDTH
IN_SPLITS = (ATT_WIDTH, ATT_WIDTH, ATT_WIDTH,
             GLA_KEY_WIDTH, GLA_KEY_WIDTH,
             GLA_WIDTH, GLA_WIDTH,
             GLA_GATE_RANK)
IN_PROJ_WIDTH = sum(IN_SPLITS)

D_FF = 5632
CONV_WIDTH = 3

kernel_name = "hybrid_chunked_attn_gla_convffn_ple"


def rms_norm(x, w):
    xf = x.astype(jnp.float32)
    y = xf * lax.rsqrt(jnp.mean(xf * xf, axis=-1, keepdims=True) + EPS)
    return (y * w.astype(jnp.float32)).astype(x.dtype)


def split_cols(t, sizes):
    idx = [int(v) for v in np.cumsum(sizes)[:-1]]
    return jnp.split(t, idx, axis=-1)


def rel_pos_bias(table):
    q_idx = jnp.arange(CHUNK) + (BAND_CHUNKS - 1) * CHUNK
    k_idx = jnp.arange(BAND_CHUNKS * CHUNK)
    dist = jnp.clip(q_idx[:, None] - k_idx[None, :], -MAX_REL_DIST, MAX_REL_DIST) + MAX_REL_DIST
    return table[:, dist]


def chunked_band_attention(q, k, v, bias_table):
    B, T, H, hd = q.shape
    nc = T // CHUNK
    pad = (BAND_CHUNKS - 1) * CHUNK
    band = BAND_CHUNKS * CHUNK
    k_pad = jnp.pad(k, ((0, 0), (pad, 0), (0, 0), (0, 0)))
    v_pad = jnp.pad(v, ((0, 0), (pad, 0), (0, 0), (0, 0)))
    bias = rel_pos_bias(bias_table).astype(jnp.float32)[None]
    key_off = jnp.arange(band) - pad
    q_chunks = q.reshape(B, nc, CHUNK, H, hd).transpose(1, 0, 2, 3, 4)
    scale = hd ** -0.5
    neg = jnp.finfo(jnp.float32).min

    def one_chunk(args):
        c, qc = args
        start = c * CHUNK
        kb = lax.dynamic_slice_in_dim(k_pad, start, band, axis=1)
        vb = lax.dynamic_slice_in_dim(v_pad, start, band, axis=1)
        s = jnp.einsum('bqhd,bkhd->bhqk', qc, kb).astype(jnp.float32) * scale + bias
        valid = (start + key_off) >= 0
        s = jnp.where(valid[None, None, None, :], s, neg)
        pr = jax.nn.softmax(s, axis=-1).astype(vb.dtype)
        return jnp.einsum('bhqk,bkhd->bqhd', pr, vb)

    out = lax.map(one_chunk, (jnp.arange(nc), q_chunks))
    return out.transpose(1, 0, 2, 3, 4).reshape(B, T, H * hd)


def gated_linear_attention(q, k, v, log_a):
    B, T, H, dk = q.shape
    dv = v.shape[-1]
    nc = T // CHUNK

    def to_chunks(t):
        return t.reshape(B, nc, CHUNK, H, t.shape[-1]).transpose(0, 1, 3, 2, 4).astype(jnp.float32)

    qc = to_chunks(q) * (dk ** -0.5)
    kc = to_chunks(k)
    vc = to_chunks(v)
    b = jnp.cumsum(to_chunks(log_a), axis=3)
    b_last = b[:, :, :, -1:, :]
    q_dec = qc * jnp.exp(b)
    k_inv = kc * jnp.exp(-b)
    k_to_end = kc * jnp.exp(b_last - b)
    causal = jnp.tril(jnp.ones((CHUNK, CHUNK), dtype=bool))
    attn = jnp.where(causal, jnp.einsum('bnhid,bnhjd->bnhij', q_dec, k_inv), 0.0)
    o_intra = jnp.einsum('bnhij,bnhjv->bnhiv', attn, vc)

    chunk_update = jnp.einsum('bnhjd,bnhjv->bnhdv', k_to_end, vc)
    chunk_decay = jnp.exp(b_last[:, :, :, 0, :])

    def step(S, xs):
        upd, dec = xs
        return dec[..., None] * S + upd, S

    S0 = jnp.zeros((B, H, dk, dv), jnp.float32)
    _, S_prev = lax.scan(step, S0, (chunk_update.transpose(1, 0, 2, 3, 4),
                                    chunk_decay.transpose(1, 0, 2, 3)))
    S_prev = S_prev.transpose(1, 0, 2, 3, 4)
    o_inter = jnp.einsum('bnhid,bnhdv->bnhiv', q_dec, S_prev)
    o = o_intra + o_inter
    return o.transpose(0, 1, 3, 2, 4).reshape(B, T, H, dv)


def causal_depthwise_conv(u, w, bias):
    T = u.shape[1]
    up = jnp.pad(u, ((0, 0), (CONV_WIDTH - 1, 0), (0, 0)))
    y = bias
    for tap in range(CONV_WIDTH):
        y = y + up[:, tap:tap + T] * w[tap]
    return y


def setup_inputs(seed: int = 0) -> dict:
    key = jax.random.key(seed)
    ks = jax.random.split(key, 20)
    f32 = jnp.float32

    def nrm(k, shape, scale):
        return jax.random.normal(k, shape, f32) * scale

    def gain(k, shape):
        return 1.0 + 0.05 * jax.random.normal(k, shape, f32)

    return {
        "x": nrm(ks[0], (BATCH, SEQ, D_MODEL), 1.0),
        "p": nrm(ks[1], (DEPTH, BATCH, SEQ, PLE_DIM), 1.0),
        "norm_mix_w": gain(ks[2], (DEPTH, D_MODEL)),
        "w_in": nrm(ks[3], (DEPTH, D_MODEL, IN_PROJ_WIDTH), D_MODEL ** -0.5),
        "att_rel_bias": nrm(ks[4], (DEPTH, ATT_HEADS, 2 * MAX_REL_DIST + 1), 0.3),
        "w_gla_gate_up": nrm(ks[5], (DEPTH, GLA_GATE_RANK, GLA_KEY_WIDTH), GLA_GATE_RANK ** -0.5),
        "b_gla_gate": nrm(ks[6], (DEPTH, GLA_KEY_WIDTH), 0.5),
        "gla_norm_w": gain(ks[7], (DEPTH, GLA_DV)),
        "w_out": nrm(ks[8], (DEPTH, MIX_WIDTH, D_MODEL), MIX_WIDTH ** -0.5),
        "norm_ffn_w": gain(ks[9], (DEPTH, D_MODEL)),
        "w_ffn_up": nrm(ks[10], (DEPTH, D_MODEL, 2 * D_FF), D_MODEL ** -0.5),
        "w_ffn_conv": nrm(ks[11], (DEPTH, CONV_WIDTH, 2 * D_FF), CONV_WIDTH ** -0.5),
        "b_ffn_conv": nrm(ks[12], (DEPTH, 2 * D_FF), 0.02),
        "w_ffn_down": nrm(ks[13], (DEPTH, D_FF, D_MODEL), D_FF ** -0.5),
        "norm_ple_w": gain(ks[14], (DEPTH, D_MODEL)),
        "w_ple_gate": nrm(ks[15], (DEPTH, D_MODEL, D_MODEL), D_MODEL ** -0.5),
        "w_ple_proj": nrm(ks[16], (DEPTH, PLE_DIM, D_MODEL), PLE_DIM ** -0.5),
        "final_norm_w": gain(ks[17], (D_MODEL,)),
    }


def reference(x, p, norm_mix_w, w_in, att_rel_bias, w_gla_gate_up, b_gla_gate, gla_norm_w,
              w_out, norm_ffn_w, w_ffn_up, w_ffn_conv, b_ffn_conv, w_ffn_down,
              norm_ple_w, w_ple_gate, w_ple_proj, final_norm_w):
    B, T, _ = x.shape
    for i in range(DEPTH):
        h = rms_norm(x, norm_mix_w[i])
        proj = h @ w_in[i]
        q_a, k_a, v_a, q_g, k_g, v_g, r_g, gate_lr = split_cols(proj, IN_SPLITS)

        hs_a = (B, T, ATT_HEADS, ATT_HEAD_DIM)
        y_a = chunked_band_attention(q_a.reshape(hs_a), k_a.reshape(hs_a), v_a.reshape(hs_a),
                                     att_rel_bias[i])

        gate_logit = (gate_lr @ w_gla_gate_up[i] + b_gla_gate[i]).astype(jnp.float32)
        log_a = (jax.nn.log_sigmoid(gate_logit) / GLA_GATE_TAU).reshape(B, T, GLA_HEADS, GLA_DK)
        o_g = gated_linear_attention(q_g.reshape(B, T, GLA_HEADS, GLA_DK),
                                     k_g.reshape(B, T, GLA_HEADS, GLA_DK),
                                     v_g.reshape(B, T, GLA_HEADS, GLA_DV), log_a)
        o_g = rms_norm(o_g.astype(x.dtype), gla_norm_w[i]).reshape(B, T, GLA_WIDTH)
        y_g = o_g * jax.nn.silu(r_g)

        x = x + jnp.concatenate([y_a, y_g], axis=-1) @ w_out[i]

        h = rms_norm(x, norm_ffn_w[i])
        u = causal_depthwise_conv(h @ w_ffn_up[i], w_ffn_conv[i], b_ffn_conv[i])
        u_gate, u_val = jnp.split(u, 2, axis=-1)
        x = x + (jax.nn.gelu(u_gate) * u_val) @ w_ffn_down[i]

        h = rms_norm(x, norm_ple_w[i])
        g = jax.nn.sigmoid(h @ w_ple_gate[i])
        x = x + g * (p[i] @ w_ple_proj[i])
    return rms_norm(x, final_norm_w)
```

```python
import math
import numpy as np
from contextlib import ExitStack
import concourse.bass as bass
import concourse.mybir as mybir
from concourse.bass_utils import run_bass_kernel_spmd

F32 = mybir.dt.float32
BF16 = mybir.dt.bfloat16
AF = mybir.ActivationFunctionType
ALU = mybir.AluOpType

D = 2048
T = 16384
NCORE = 8
TOK = T // NCORE
NPRE_ST = 28
PREF = NPRE_ST * 512
MAIN_TILES = 17
MAIN = MAIN_TILES * 128
XC = PREF + MAIN
EPS = 1e-6
NEG = -30000.0
DFF = 5632
NFC = DFF // 128


class Buf:
    __slots__ = ("name", "w", "r", "psum")

    def __init__(self, name="", psum=False):
        self.name = name
        self.w = None
        self.r = {}
        self.psum = psum


class Track:
    def __init__(self, sem, name):
        self.sem = sem
        self.count = 0
        self.name = name


class Eng:
    def __init__(self, k, name, eng):
        self.name = name
        self.eng = eng
        self.track = Track(k.new_sem("e_" + name), name)
        self.waited = {}
        self.nwaits = 0
        self.nops = 0


class K:
    def __init__(self, nc, stack):
        self.nc = nc
        self.stack = stack
        self.pe = Eng(self, "pe", nc.tensor)
        self.act = Eng(self, "act", nc.scalar)
        self.dve = Eng(self, "dve", nc.vector)
        self.pool = Eng(self, "pool", nc.gpsimd)
        self.sp = Eng(self, "sp", nc.sync)
        self.engs = [self.pe, self.act, self.dve, self.pool, self.sp]

    def new_sem(self, name):
        return self.stack.enter_context(self.nc.semaphore(name))

    def dma_track(self, name):
        return Track(self.new_sem("d_" + name), name)

    def sb(self, name, shape, dtype):
        return self.stack.enter_context(self.nc.sbuf_tensor("s_" + name, list(shape), dtype))

    def ps(self, name, shape, dtype=F32):
        return self.stack.enter_context(self.nc.psum_tensor("p_" + name, list(shape), dtype))

    def _deps(self, e, reads, writes, skip_own=False):
        need = {}

        def add(tc):
            if tc is None:
                return
            t, c = tc
            if need.get(t, 0) < c:
                need[t] = c

        for b in reads:
            add(b.w)
            if b.psum:
                for t, c in b.r.items():
                    if t is not e.track:
                        add((t, c))
        for b in writes:
            add(b.w)
            for t, c in b.r.items():
                add((t, c))
        for t, c in need.items():
            if skip_own and t is e.track:
                continue
            if e.waited.get(t, 0) >= c:
                continue
            e.eng.wait_ge(t.sem, c)
            e.waited[t] = c
            e.nwaits += 1

    def op(self, e, fn, reads=(), writes=(), pe_acc=False):
        self._deps(e, reads, writes, skip_own=(pe_acc or e is self.pe))
        ins = fn()
        t = e.track
        t.count += 1
        ins.then_inc(t.sem, 1)
        e.nops += 1
        for b in reads:
            b.r[t] = t.count
        for b in writes:
            b.w = (t, t.count)
            b.r = {}
        return ins

    def dma(self, e, track, out, in_, reads=(), writes=()):
        self._deps(e, reads, writes)
        ins = e.eng.dma_start(out=out, in_=in_)
        track.count += 16
        ins.then_inc(track.sem, 16)
        for b in reads:
            b.r[track] = track.count
        for b in writes:
            b.w = (track, track.count)
            b.r = {}
        return ins


class _Stop(Exception):
    pass


def build(dbg=None, npre_run=None, max_items=None, stop=None):
    nc = bass.Bass("TRN2", target_bir_lowering=False)

    def din(name, shape):
        return nc.dram_tensor(name, list(shape), F32, kind="ExternalInput").ap()

    xc = din("xc", [XC, D])
    pc = din("pc", [MAIN, 256])
    keymask_d = din("keymask", [128, 4 + MAIN_TILES])
    biasT_d = din("biasT", [16, 128, 640])
    ident_d = din("ident", [128, 128])
    maskT_d = din("maskT", [128, 128])
    triu_d = din("triu", [128, 128])
    tril_d = din("tril", [128, 128])
    nmix_d = din("nmix", [128, 16])
    nffn_d = din("nffn", [128, 16])
    nple_d = din("nple", [128, 16])
    convw_d = din("convw", [128, 3, 88])
    convb_d = din("convb", [128, 88])
    gnw_d = din("gnw", [128, 256])
    fnw_d = din("fnw", [128, D])
    wup_d = din("wupaug", [32, 512])
    w_in = din("w_in", [D, 6160]).rearrange("(c p) n -> p c n", p=128)
    w_out = din("w_out", [D, D]).rearrange("(c p) n -> p c n", p=128)
    w_up = din("w_ffn_up", [D, 2 * DFF]).rearrange("(c p) n -> p c n", p=128)
    w_down = din("w_ffn_down", [DFF, D]).rearrange("(c p) n -> p c n", p=128)
    w_pg = din("w_ple_gate", [D, D]).rearrange("(c p) n -> p c n", p=128)
    w_pp = din("w_ple_proj", [256, D]).rearrange("(c p) n -> p c n", p=128)
    out_d = nc.dram_tensor("out", [TOK, D], F32, kind="ExternalOutput").ap()
    dbg_outs = {}

    with ExitStack() as st:
        k = K(nc, st)
        PE, ACT_, DVE, POOL, SP = k.pe, k.act, k.dve, k.pool, k.sp

        def ckg(n):
            if stop == n:
                raise _Stop()

        def A(out, in_, func, r, w, **kw):
            return k.op(ACT_, lambda: nc.scalar.activation(out=out, in_=in_, func=func, **kw), r, w)

        def veng(e):
            return nc.vector if e is DVE else nc.gpsimd

        def TS(e, out, in0, s1, s2, op0, op1, r, w):
            if s2 is None:
                return k.op(e, lambda: veng(e).tensor_scalar(out=out, in0=in0, scalar1=s1, scalar2=None, op0=op0), r, w)
            return k.op(e, lambda: veng(e).tensor_scalar(out=out, in0=in0, scalar1=s1, scalar2=s2, op0=op0, op1=op1), r, w)

        def TT(e, out, in0, in1, op, r, w):
            return k.op(e, lambda: veng(e).tensor_tensor(out=out, in0=in0, in1=in1, op=op), r, w)

        def STT(e, out, in0, scalar, in1, op0, op1, r, w):
            return k.op(e, lambda: veng(e).scalar_tensor_tensor(out=out, in0=in0, scalar=scalar, in1=in1, op0=op0, op1=op1), r, w)

        def CP(e, out, in_, r, w):
            if e is ACT_:
                return k.op(e, lambda: nc.scalar.copy(out=out, in_=in_), r, w)
            return k.op(e, lambda: veng(e).tensor_copy(out=out, in_=in_), r, w)

        def MS(e, ap, val, w):
            return k.op(e, lambda: veng(e).memset(ap, val), (), w)

        def MM(out, lhsT, rhs, start, stop, r, w):
            return k.op(PE, lambda: nc.tensor.matmul(out, lhsT=lhsT, rhs=rhs, start=start, stop=stop), r, w,
                        pe_acc=not start)

        def RSTD(dst, dstB, src, srcB):
            ckg(10)
            TS(DVE, dst[:], src[:], EPS, None, ALU.add, None, [srcB], [dstB])
            ckg(11)
            A(dst[:], dst[:], AF.Ln, [dstB], [dstB])
            ckg(12)
            A(dst[:], dst[:], AF.Exp, [dstB], [dstB], scale=-0.5)
            ckg(13)

        def TR(out, in_, r, w):
            return k.op(PE, lambda: nc.tensor.transpose(out, in_, ident[:]), list(r) + [constSB], w)

        pf = [k.ps(f"pf{i}", [128, 512], F32) for i in range(6)]
        pfB = [Buf(f"pf{i}", psum=True) for i in range(6)]
        pb = [k.ps(f"pb{i}", [128, 1024], BF16) for i in range(2)]
        pbB = [Buf(f"pb{i}", psum=True) for i in range(2)]
        cnt = {"f": 0, "b": 0, "w": 0, "pt": 0}

        def nf():
            i = cnt["f"] % 4
            cnt["f"] += 1
            return pf[i], pfB[i]

        def nb():
            i = cnt["b"] % 2
            cnt["b"] += 1
            return pb[i], pbB[i]

        constB = Buf("const")
        tconst = k.dma_track("const")
        ident = k.sb("ident", [128, 128], BF16)
        maskT = k.sb("maskT", [128, 128], F32)
        triu = k.sb("triu", [128, 128], F32)
        tril = k.sb("tril", [128, 128], F32)
        nmix = k.sb("nmix", [128, 16], F32)
        nffn = k.sb("nffn", [128, 16], F32)
        nple = k.sb("nple", [128, 16], F32)
        convw = k.sb("convw", [128, 3, 88], F32)
        convb = k.sb("convb", [128, 88], F32)
        gnw = k.sb("gnw", [128, 256], F32)
        keymask = k.sb("keymask", [128, 4 + MAIN_TILES], F32)
        wupaug = k.sb("wupaug", [32, 512], BF16)
        constSB = Buf("constS")
        tconsts = k.dma_track("consts")
        k.dma(POOL, tconsts, ident[:], ident_d, writes=[constSB])
        k.dma(POOL, tconsts, wupaug[:], wup_d, writes=[constSB])
        for dst, src in [(maskT, maskT_d), (triu, triu_d), (tril, tril_d), (nmix, nmix_d), (nffn, nffn_d),
                         (nple, nple_d), (convw, convw_d), (convb, convb_d), (gnw, gnw_d), (keymask, keymask_d)]:
            k.dma(SP, tconst, dst[:], src, writes=[constB])

        S = k.sb("S", [128, 4, 256], F32)
        SB_ = Buf("S")
        MS(DVE, S[:], 0.0, [SB_])
        hT = k.sb("hT", [128, 16, 512], BF16)
        hTB = [Buf(f"hT{c}") for c in range(16)]
        htmp = k.sb("htmp", [128, D], BF16)
        htmpB = Buf("htmp")
        ss = k.sb("ss", [128, 1], F32)
        ssB = Buf("ss")
        rstd = k.sb("rstd", [128, 1], F32)
        rstdB = Buf("rstd")
        glT = k.sb("glT", [32, 512], BF16)
        glTB = Buf("glT")
        MS(DVE, glT[:], 1.0, [glTB])
        la = k.sb("la", [128, 4, 512], F32)
        laB = [Buf(f"la{t}") for t in range(4)]
        gtmp = k.sb("gtmp", [128, 512], F32)
        gtmpB = Buf("gtmp")

        def dbg_dump(name, ap, shape, rbufs, dtype=F32):
            if dbg is None or name not in dbg:
                return
            d = nc.dram_tensor("dbg_" + name, list(shape), dtype, kind="ExternalOutput").ap()
            tr = k.dma_track("dbg_" + name)
            k.dma(SP, tr, d, ap, reads=rbufs)
            dbg_outs[name] = tr

        class NS:
            pass
        M = NS()
        M.hT, M.hTB, M.htmp, M.htmpB, M.ss, M.ssB, M.rstd, M.rstdB = hT, hTB, htmp, htmpB, ss, ssB, rstd, rstdB
        M.la, M.laB, M.glT, M.glTB, M.gtmp, M.gtmpB = la, laB, glT, glTB, gtmp, gtmpB
        M.htmpL, M.htmpLB = [htmp], [htmpB]

        def norm_partA(xa, xB, ns, hi=0):
            htmp_, htmpB_ = ns.htmpL[hi], ns.htmpLB[hi]
            MS(DVE, ns.ss[:], 0.0, [ns.ssB])
            A(htmp_[:], xa, AF.Square, [xB], [htmpB_, ns.ssB], scale=1.0 / math.sqrt(D), accum_out=ns.ss[:])
            RSTD(ns.rstd, ns.rstdB, ns.ss, ns.ssB)
            TS(DVE, htmp_[:], xa, ns.rstd[:, 0:1], None, ALU.mult, None, [xB, ns.rstdB], [htmpB_])

        def norm_partB(t, wcol, ns, hi=0):
            htmp_, htmpB_ = ns.htmpL[hi], ns.htmpLB[hi]
            for half in range(2):
                bk, bB = nb()
                for j in range(8):
                    c = half * 8 + j
                    TR(bk[:, j * 128:(j + 1) * 128], htmp_[:, c * 128:(c + 1) * 128], [htmpB_, constB], [bB])
                for j in range(8):
                    c = half * 8 + j
                    dst = ns.hT[:, c, t * 128:(t + 1) * 128]
                    src = bk[:, j * 128:(j + 1) * 128]
                    if half == 0:
                        A(dst, src, AF.Copy, [bB, constB], [ns.hTB[c]], scale=wcol[:, c:c + 1])
                    else:
                        TS(DVE, dst, src, wcol[:, c:c + 1], None, ALU.mult, None, [bB, constB], [ns.hTB[c]])

        def norm_tile(xa, xB, t, wcol, ns):
            norm_partA(xa, xB, ns)
            norm_partB(t, wcol, ns)

        def norm_hT(xs, wcol, nt, ns=None):
            ns = ns or M
            junk = gact[:, 0:4, :].rearrange("p a b -> p (a b)")
            junkB = gactB[0:4]

            def a1(t):
                xa, xB = xs[t]
                MS(DVE, ns.ss[:], 0.0, [ns.ssB])
                A(junk, xa, AF.Square, [xB], junkB + [ns.ssB], scale=1.0 / math.sqrt(D), accum_out=ns.ss[:])
                RSTD(ns.rstd, ns.rstdB, ns.ss, ns.ssB)

            def a2(t):
                xa, xB = xs[t]
                TS(DVE, ns.htmp[:], xa, ns.rstd[:, 0:1], None, ALU.mult, None, [xB, ns.rstdB], [ns.htmpB])
            a1(0)
            a2(0)
            for t in range(nt):
                if t + 1 < nt:
                    a1(t + 1)
                norm_partB(t, wcol, ns)
                if t + 1 < nt:
                    a2(t + 1)

        def gate_g1(wg, wgB, nt, ns):
            N = nt * 128
            bk, bB = nf()
            for c in range(16):
                MM(bk[0:16, 0:N], wg[:, c, 0:16], ns.hT[:, c, 0:N], c == 0, c == 15, [wgB, ns.hTB[c]], [bB])
            CP(ACT_, ns.glT[0:16, 0:N], bk[0:16, 0:N], [bB], [ns.glTB])

        def gate_g2(nt, ns):
            for t in range(nt):
                bk, bB = nf()
                MM(bk[:, :], ns.glT[0:32, t * 128:(t + 1) * 128], wupaug[0:32, :], True, True, [ns.glTB, constSB], [bB])
                A(ns.gtmp[:], bk[:, :], AF.Exp, [bB], [ns.gtmpB], scale=-1.0)
                A(ns.la[:, t, :], ns.gtmp[:], AF.Ln, [ns.gtmpB], [ns.laB[t]], bias=1.0)

        def gate_chain(wg, wgB, nt, ns=None):
            ns = ns or M
            gate_g1(wg, wgB, nt, ns)
            gate_g2(nt, ns)

        HALF = NFC // 2
        scr = {}
        items = []

        seqc = {"n": 0, "first": True, "use": False}

        def add(parts, fn, skip=False):
            if skip:
                if parts is not None and seqc["use"]:
                    seqc["n"] += 1
                return
            if parts is None or not seqc["use"]:
                items.append((parts, fn, None, True))
            else:
                items.append((parts, fn, seqc["n"], seqc["first"]))
                seqc["n"] += 1

        def main_st(tile0, nt, store):
            N = nt * 128
            r0 = PREF + tile0 * 128
            seqc["n"] = 0
            seqc["first"] = (tile0 == 1)
            seqc["use"] = True

            def stage_load():
                for t in range(nt):
                    k.dma(SP, xT_[t], xres[t][:], xc[r0 + t * 128:r0 + (t + 1) * 128, :], writes=[xresB[t]])
                norm_hT([(xres[t][:], xresB[t]) for t in range(nt)], nmix, nt)
            add(None, stage_load)
            for p in range(8):
                add([w_in[:, :, 128 * p:128 * p + 128], w_in[:, :, 1024 + 128 * p:1024 + 128 * p + 128],
                     w_in[:, :, 2048 + 128 * p:2048 + 128 * p + 128]],
                    lambda v, vB, p=p: attention_pair(p, wjoin(v), vB, nt, tile0))
            add([w_in[:, :, 6144:6160]], lambda v, vB: gate_chain(v[0], vB, nt))
            for h in range(4):
                add([w_in[:, :, 3072 + 128 * h:3072 + 128 * h + 128], w_in[:, :, 3584 + 128 * h:3584 + 128 * h + 128]],
                    lambda v, vB, h=h: gla_head_a(h, wjoin(v), vB, nt))
                add([w_in[:, :, 4096 + 256 * h:4096 + 256 * h + 256], w_in[:, :, 5120 + 256 * h:5120 + 256 * h + 256]],
                    lambda v, vB, h=h: gla_head_b(h, wjoin(v), vB, nt))
            for g in range(4):
                def f_out(v, vB, g=g):
                    for t in range(nt):
                        bk, bB = nf()
                        for c in range(16):
                            MM(bk[:, :], yT[:, c, t * 128:(t + 1) * 128], v[0][:, c, :], c == 0, c == 15, [yTB[c], vB], [bB])
                        TT(DVE, xres[t][:, g * 512:(g + 1) * 512], xres[t][:, g * 512:(g + 1) * 512], bk[:, :], ALU.add,
                           [xresB[t], bB], [xresB[t]])
                add([w_out[:, :, g * 512:(g + 1) * 512]], f_out)
            add(None, lambda: norm_hT([(xres[t][:], xresB[t]) for t in range(nt)], nffn, nt))
            for half in range(2):
                for blk in range(HALF // 2):
                    def f_up(v, vB, half=half, blk=blk):
                        for jj in range(2):
                            jq = blk * 2 + jj
                            j = half * HALF + jq
                            for br in range(2):
                                ch = j + br * NFC
                                bk, bB = nf()
                                for c in range(16):
                                    MM(bk[:, 0:N], v[br][:, c, jj * 128:(jj + 1) * 128], hT[:, c, 0:N], c == 0, c == 15,
                                       [vB, hTB[c]], [bB])
                                CP(POOL, ext[br][:, 0:2], tail[:, ch, :], [tailB], [extB[br]])
                                CP(ACT_, ext[br][:, 2:2 + N], bk[:, 0:N], [bB], [extB[br]])
                                CP(POOL, tail[:, ch, :], ext[br][:, N:N + 2], [extB[br]], [tailB])
                                TS(DVE, ycv[br][:, 0:N], ext[br][:, 2:2 + N], convw[:, 2, ch:ch + 1], convb[:, ch:ch + 1],
                                   ALU.mult, ALU.add, [extB[br], constB], [ycvB[br]])
                                STT(DVE, ycv[br][:, 0:N], ext[br][:, 1:1 + N], convw[:, 1, ch:ch + 1], ycv[br][:, 0:N],
                                    ALU.mult, ALU.add, [extB[br], constB, ycvB[br]], [ycvB[br]])
                                STT(DVE, ycv[br][:, 0:N], ext[br][:, 0:N], convw[:, 0, ch:ch + 1], ycv[br][:, 0:N],
                                    ALU.mult, ALU.add, [extB[br], constB, ycvB[br]], [ycvB[br]])
                            A(ycv[0][:, 0:N], ycv[0][:, 0:N], AF.Gelu, [ycvB[0]], [ycvB[0]])
                            TT(DVE, gact[:, jq, 0:N], ycv[0][:, 0:N], ycv[1][:, 0:N], ALU.mult, [ycvB[0], ycvB[1]],
                               [gactB[jq]])
                    c0 = (half * HALF + blk * 2) * 128
                    add([w_up[:, :, c0:c0 + 256], w_up[:, :, DFF + c0:DFF + c0 + 256]], f_up)
                banks = {}
                for g in range(4):
                    for kg in range(2):
                        def f_dn(v, vB, g=g, kg=kg, half=half):
                            for t in range(nt):
                                if kg == 0:
                                    banks[(g, t)] = nf()
                                bk, bB = banks[(g, t)]
                                for jj in range(11):
                                    jq = kg * 11 + jj
                                    MM(bk[:, :], gact[:, jq, t * 128:(t + 1) * 128], v[0][:, jj, :],
                                       kg == 0 and jj == 0, kg == 1 and jj == 10, [gactB[jq], vB], [bB])
                                if kg == 1:
                                    TT(DVE, xres[t][:, g * 512:(g + 1) * 512], xres[t][:, g * 512:(g + 1) * 512], bk[:, :],
                                       ALU.add, [xresB[t], bB], [xresB[t]])
                        f0 = half * HALF + kg * 11
                        add([w_down[:, f0:f0 + 11, g * 512:(g + 1) * 512]], f_dn, skip=not store)
            def stage_ple():
                norm_hT([(xres[t][:], xresB[t]) for t in range(nt)], nple, nt)
                bk, bB = nb()
                for t in range(nt):
                    pr0 = tile0 * 128 + t * 128
                    k.dma(SP, pinT, pin[:], pc[pr0:pr0 + 128, :], writes=[pinB])
                    CP(DVE, pbf[:], pin[:], [pinB], [pbfB])
                    for kc in range(2):
                        TR(bk[:, kc * 512 + t * 128:kc * 512 + (t + 1) * 128], pbf[:, kc * 128:(kc + 1) * 128],
                           [pbfB, constB], [bB])
                for kc in range(2):
                    CP(ACT_, pT[:, kc, 0:N], bk[:, kc * 512:kc * 512 + N], [bB], [pTB])
            add(None, stage_ple, skip=not store)
            for g in range(8):
                def f_ple(v, vB, g=g):
                    for t in range(nt):
                        bk, bB = nf()
                        for c in range(16):
                            MM(bk[:, 0:256], hT[:, c, t * 128:(t + 1) * 128], v[0][:, c, :], c == 0, c == 15, [hTB[c], vB], [bB])
                        A(gsig[:, 0:256], bk[:, 0:256], AF.Sigmoid, [bB], [gsigB])
                        b2, b2B = nf()
                        for kc in range(2):
                            MM(b2[:, 0:256], pT[:, kc, t * 128:(t + 1) * 128], v[1][:, kc, :], kc == 0, kc == 1, [pTB, vB], [b2B])
                        TT(DVE, gsig[:, 0:256], gsig[:, 0:256], b2[:, 0:256], ALU.mult, [gsigB, b2B], [gsigB])
                        TT(DVE, xres[t][:, g * 256:(g + 1) * 256], xres[t][:, g * 256:(g + 1) * 256], gsig[:, 0:256], ALU.add,
                           [xresB[t], gsigB], [xresB[t]])
                add([w_pg[:, :, g * 256:(g + 1) * 256], w_pp[:, :, g * 256:(g + 1) * 256]], f_ple, skip=not store)
            def stage_final():
                if not store:
                    return
                for t in range(nt):
                    MS(DVE, ss[:], 0.0, [ssB])
                    A(htmp[:], xres[t][:], AF.Square, [xresB[t]], [htmpB, ssB], scale=1.0 / math.sqrt(D), accum_out=ss[:])
                    RSTD(rstd, rstdB, ss, ssB)
                    STT(DVE, xres[t][:], xres[t][:], rstd[:, 0:1], fnw[:], ALU.mult, ALU.mult, [xresB[t], rstdB, fnwB],
                        [xresB[t]])
                    o0 = (tile0 - 1 + t) * 128
                    k.dma(SP, oT[t], out_d[o0:o0 + 128, :], xres[t][:], reads=[xresB[t]])
            add(None, stage_final, skip=not store)

        def wjoin(v):
            return WJ(v)

        class WJ:
            def __init__(self, parts):
                self.parts = parts
                self.offs = []
                o = 0
                for pp in parts:
                    self.offs.append(o)
                    o += pp.shape[2]
                self.n = o

            def __getitem__(self, key):
                ps_, c, cols = key
                a, b = cols.start, cols.stop
                for pp, o in zip(self.parts, self.offs):
                    if a >= o and b <= o + pp.shape[2]:
                        return pp[ps_, c, a - o:b - o]
                raise IndexError((a, b))

        def stage_halo():
            for t in range(4):
                r0 = PREF - 512 + t * 128
                k.dma(SP, xT_[t], xres[t][:], xc[r0:r0 + 128, :], writes=[xresB[t]])
            norm_hT([(xres[t][:], xresB[t]) for t in range(4)], nmix, 4)
        add(None, stage_halo)
        for p in range(8):
            def f_halo(v, vB, p=p):
                wv_ = WJ([v[0], v[0], v[1]])
                attn_kv(p, wv_, vB, 4, kTprev[:, p, :], kTprevB[p], lambda t: Vprev[:, t, 2 * p:2 * p + 2, 0:64], VprevB[p])
            add([w_in[:, :, 1024 + 128 * p:1024 + 128 * p + 128], w_in[:, :, 2048 + 128 * p:2048 + 128 * p + 128]], f_halo)
        main_st(0, 1, False)
        for s_ in range(4):
            main_st(1 + 4 * s_, 4, True)

        with ExitStack() as pst:
            def psb(name, shape, dtype):
                return pst.enter_context(nc.sbuf_tensor("s_" + name, list(shape), dtype))

            wpre = psb("wpre", [128, 16, 1552], BF16)
            wpreB = Buf("wpre")
            twp = k.dma_track("wpre")
            k.dma(POOL, twp, wpre[:, :, 0:512], w_in[:, :, 3584:4096], writes=[wpreB])
            k.dma(POOL, twp, wpre[:, :, 512:1024], w_in[:, :, 4096:4608], writes=[wpreB])
            k.dma(POOL, twp, wpre[:, :, 1024:1536], w_in[:, :, 4608:5120], writes=[wpreB])
            k.dma(POOL, twp, wpre[:, :, 1536:1552], w_in[:, :, 6144:6160], writes=[wpreB])
            NBG = 8
            tbg = [k.dma_track(f"bg{i}") for i in range(NBG)]
            for (parts_, fn_, key_, first_) in items:
                if key_ is None or not first_:
                    continue
                n_ = sum(pp.shape[1] * pp.shape[2] for pp in parts_)
                d_ = nc.dram_tensor(f"scr{key_}", [128, n_], BF16).ap()
                off_ = 0
                for src_ in parts_:
                    nch_, ncol_ = src_.shape[1], src_.shape[2]
                    k.dma(POOL, tbg[key_ % NBG], d_[:, off_:off_ + nch_ * ncol_].rearrange("p (c n) -> p c n", c=nch_), src_)
                    off_ += nch_ * ncol_
                scr[key_] = d_
            NXP = 4
            xp = [psb(f"xp{i}", [128, D], F32) for i in range(NXP)]
            xpB = [Buf(f"xp{i}") for i in range(NXP)]
            xpT = [k.dma_track(f"xp{i}") for i in range(NXP)]
            PS = []
            for par in range(2):
                ns = NS()
                if par == 0:
                    ns.__dict__.update(M.__dict__)
                    ns.htmpL = [htmp, psb("htmp_a1", [128, D], BF16)]
                    ns.htmpLB = [htmpB, Buf("htmp_a1")]
                else:
                    ns.hT = psb("hT_b", [128, 16, 512], BF16)
                    ns.hTB = [Buf(f"hTb{c}") for c in range(16)]
                    ns.htmpL = [psb(f"htmp_b{i}", [128, D], BF16) for i in range(2)]
                    ns.htmpLB = [Buf(f"htmp_b{i}") for i in range(2)]
                    ns.ss = psb("ss_b", [128, 1], F32)
                    ns.ssB = Buf("ss_b")
                    ns.rstd = psb("rstd_b", [128, 1], F32)
                    ns.rstdB = Buf("rstd_b")
                    ns.la = psb("la_b", [128, 4, 512], F32)
                    ns.laB = [Buf(f"la_b{t}") for t in range(4)]
                    ns.glT = psb("glT_b", [32, 512], BF16)
                    ns.glTB = Buf("glT_b")
                    MS(DVE, ns.glT[:], 1.0, [ns.glTB])
                    ns.gtmp = psb("gtmp_b", [128, 512], F32)
                    ns.gtmpB = Buf("gtmp_b")
                ns.ee2 = [psb(f"ee2_{par}{i}", [128, 512], F32) for i in range(2)]
                ns.ee2B = [Buf(f"ee2_{par}{i}") for i in range(2)]
                ns.decp = [psb(f"decp_{par}{i}", [128, 4], F32) for i in range(2)]
                ns.decpB = [Buf(f"decp_{par}{i}") for i in range(2)]
                ns.kte = [psb(f"kte_{par}{i}", [128, 512], BF16) for i in range(2)]
                ns.kteB = [Buf(f"kte_{par}{i}") for i in range(2)]
                ns.vpre = [psb(f"vpre_{par}{i}", [128, 1024], BF16) for i in range(2)]
                ns.vpreB = [Buf(f"vpre_{par}{i}") for i in range(2)]
                PS.append(ns)
            xcnt = 0
            wgp = wpre[:, :, 1536:1552]

            def ck(n):
                if stop == n:
                    raise _Stop()

            def pA(sti, t, ns, hi):
                nonlocal xcnt
                i = xcnt % NXP
                xcnt += 1
                r0 = sti * 512 + t * 128
                k.dma(SP, xpT[i], xp[i][:], xc[r0:r0 + 128, :], writes=[xpB[i]])
                norm_partA(xp[i][:], xpB[i], ns, hi)

            def mm_tile(t, ns):
                tsl = slice(t * 128, (t + 1) * 128)
                q = t % 2
                bk, bB = nf()
                MM(bk[:, :], tril[:, :], ns.la[:, t, :], True, True, [constB, ns.laB[t]], [bB])
                A(ns.ee2[q][:], bk[:, :], AF.Exp, [bB], [ns.ee2B[q]])
                bk, bB = nf()
                for h in range(4):
                    MM(bk[:, h:h + 1], ns.la[:, t, h * 128:(h + 1) * 128], triu[:, 127:128], True, True,
                       [constB, ns.laB[t]], [bB])
                A(ns.decp[q][:], bk[:, 0:4], AF.Exp, [bB], [ns.decpB[q]])
                bk, bB = nf()
                for c in range(16):
                    MM(bk[:, :], ns.hT[:, c, tsl], wpre[:, c, 0:512], c == 0, c == 15, [ns.hTB[c], wpreB], [bB])
                TT(DVE, ns.kte[q][:], bk[:, :], ns.ee2[q][:], ALU.mult, [bB, ns.ee2B[q]], [ns.kteB[q]])
                for vh in range(2):
                    bk, bB = nf()
                    for c in range(16):
                        MM(bk[:, :], ns.hT[:, c, tsl], wpre[:, c, 512 + vh * 512:1024 + vh * 512], c == 0, c == 15,
                           [ns.hTB[c], wpreB], [bB])
                    CP(ACT_, ns.vpre[q][:, vh * 512:(vh + 1) * 512], bk[:, :], [bB], [ns.vpreB[q]])
                for h in range(4):
                    bk, bB = nf()
                    MM(bk[:, 0:256], ns.kte[q][:, h * 128:(h + 1) * 128], ns.vpre[q][:, h * 256:(h + 1) * 256], True, True,
                       [ns.kteB[q], ns.vpreB[q]], [bB])
                    STT(DVE, S[:, h, :], S[:, h, :], ns.decp[q][:, h:h + 1], bk[:, 0:256], ALU.mult, ALU.add,
                        [SB_, ns.decpB[q], bB], [SB_])
            try:
                sts = list(range(NPRE_ST - (NPRE_ST if npre_run is None else npre_run), NPRE_ST))
                if sts:
                    ns0 = PS[0]
                    for t in range(4):
                        pA(sts[0], t, ns0, t % 2)
                        norm_partB(t, nmix, ns0, t % 2)
                    gate_chain(wgp, wpreB, 4, ns0)
                sched = {0: [(0, 0), (1, 1)], 1: [(2, 0)], 2: [(3, 1)], 3: []}
                for si, sti in enumerate(sts):
                    ns = PS[si % 2]
                    nn = PS[(si + 1) % 2]
                    nxt_ = sts[si + 1] if si + 1 < len(sts) else None
                    for t in range(4):
                        if nxt_ is not None:
                            for (tt, hi) in sched[t]:
                                pA(nxt_, tt, nn, hi)
                            if t == 3:
                                gate_g1(wgp, wpreB, 4, nn)
                        mm_tile(t, ns)
                        if nxt_ is not None:
                            for (tt, hi) in sched[t]:
                                norm_partB(tt, nmix, nn, hi)
                            if t == 3:
                                gate_g2(4, nn)
            except _Stop:
                pass
            for e in k.engs:
                for o in k.engs:
                    if o.track.count > e.waited.get(o.track, 0):
                        e.eng.wait_ge(o.track.sem, o.track.count)
                        e.waited[o.track] = o.track.count
                for tr in xpT + [twp] + tbg:
                    if tr.count > e.waited.get(tr, 0):
                        e.eng.wait_ge(tr.sem, tr.count)
                        e.waited[tr] = tr.count
        dbg_dump("S_in", S[:], [128, 4, 256], [SB_])

        NW = 2
        WSZ = 8192
        wbuf = [k.sb(f"wbuf{i}", [128, WSZ], BF16) for i in range(NW)]
        wB = [Buf(f"wbuf{i}") for i in range(NW)]
        wT = [k.dma_track(f"w{i}") for i in range(NW)]

        wT2 = [k.dma_track(f"wh{i}") for i in range(NW)]

        def load_w(parts, key=None, first=True):
            i = cnt["w"] % NW
            cnt["w"] += 1
            off = 0
            views = []
            for src in parts:
                nch, ncol = src.shape[1], src.shape[2]
                views.append(wbuf[i][:, off:off + nch * ncol].rearrange("p (c n) -> p c n", c=nch))
                off += nch * ncol
            assert off <= WSZ
            if key is None:
                for src, dst in zip(parts, views):
                    k.dma(POOL, wT[i], dst, src, writes=[wB[i]])
            else:
                k.dma(SP, wT2[i], wbuf[i][:, 0:off], scr[key], writes=[wB[i]])
            return views, wB[i], i, off

        xres = [k.sb(f"xres{t}", [128, D], F32) for t in range(4)]
        xresB = [Buf(f"xres{t}") for t in range(4)]
        xT_ = [k.dma_track(f"xres{t}") for t in range(4)]
        yT = k.sb("yT", [128, 16, 512], BF16)
        yTB = [Buf(f"yT{c}") for c in range(16)]
        kTprev = k.sb("kTprev", [128, 8, 512], BF16)
        kTprevB = [Buf(f"kTprev{p}") for p in range(8)]
        Vprev = k.sb("Vprev", [128, 4, 16, 65], BF16)
        VprevB = [Buf(f"Vprev{p}") for p in range(8)]
        MS(DVE, Vprev[:], 1.0, VprevB)
        qT = k.sb("qT", [128, 512], BF16)
        qTB = Buf("qT")
        kTcur = k.sb("kTcur", [128, 512], BF16)
        kTcurB = Buf("kTcur")
        Vcur = k.sb("Vcur", [128, 4, 2, 65], BF16)
        VcurB = Buf("Vcur")
        MS(DVE, Vcur[:], 1.0, [VcurB])
        biasb = [k.sb(f"biasb{i}", [128, 640], F32) for i in range(2)]
        biasB = [Buf(f"biasb{i}") for i in range(2)]
        biasTr = [k.dma_track(f"bias{i}") for i in range(2)]
        NPT = 4
        PT = [k.sb(f"PT{i}", [128, 512], BF16) for i in range(NPT)]
        PTB = [Buf(f"PT{i}") for i in range(NPT)]
        rec = k.sb("rec", [128, 2, 4], F32)
        recB = [Buf("rec0"), Buf("rec1")]
        eb = k.sb("eb", [128, 512], F32)
        ebB = Buf("eb")
        enb = k.sb("enb", [128, 512], F32)
        enbB = Buf("enb")
        ee2m = k.sb("ee2m", [128, 512], F32)
        ee2mB = Buf("ee2m")
        qdec = k.sb("qdec", [128, 512], BF16)
        qdecB = Buf("qdec")
        ypair = qdec[:].rearrange("p (t d) -> p t d", d=128)
        ypairB = qdecB
        kinv = kTcur
        kinvB = kTcurB
        ktem = k.sb("ktem", [128, 4, 128], BF16)
        ktemB = Buf("ktem")
        vg = k.sb("vg", [128, 4, 256], BF16)
        vgB = Buf("vg")
        silr = k.sb("silr", [128, 4, 256], F32)
        silrB = Buf("silr")
        Sbf4 = k.sb("Sbf4", [128, 4, 256], BF16)
        Sbf4B = [Buf(f"Sbf4_{t}") for t in range(4)]
        attT4 = k.sb("attT4", [128, 4, 128], BF16)
        attT4B = [Buf(f"attT4_{t}") for t in range(4)]
        otmp = k.sb("otmp", [128, 256], F32)
        otmpB = Buf("otmp")
        yg4 = k.sb("yg4", [128, 4, 256], BF16)
        yg4B = [Buf(f"yg4_{t}") for t in range(4)]
        yg = yg4[:, 0, :]
        ygB = yg4B[0]
        pend_tr = {}
        ss2 = k.sb("ss2", [128, 1], F32)
        ss2B = Buf("ss2")
        rstd2 = k.sb("rstd2", [128, 1], F32)
        rstd2B = Buf("rstd2")
        HALF = NFC // 2
        gact = k.sb("gact", [128, HALF, 512], BF16)
        gactB = [Buf(f"gact{j}") for j in range(HALF)]
        ext = [k.sb(f"ext{i}", [128, 516], F32) for i in range(2)]
        extB = [Buf(f"ext{i}") for i in range(2)]
        ycv = [gtmp, k.sb("ycv1", [128, 512], F32)]
        ycvB = [gtmpB, Buf("ycv1")]
        tail = k.sb("tail", [128, 88, 2], F32)
        tailB = Buf("tail")
        MS(DVE, tail[:], 0.0, [tailB])
        pin = otmp
        pinB = otmpB
        pinT = k.dma_track("pin")
        pbf = yg
        pbfB = ygB
        pT = k.sb("pT", [128, 2, 512], BF16)
        pTB = Buf("pT")
        gsig = ycv[1]
        gsigB = ycvB[1]
        fnw = k.sb("fnw", [128, D], F32)
        fnwB = Buf("fnw")
        k.dma(SP, tconst, fnw[:], fnw_d, writes=[fnwB])
        oT = [k.dma_track(f"out{t}") for t in range(4)]

        def attn_kv(p, wv_, wvB, nt, kdst, kdstB, vdst_fn, vdstB):
            N = nt * 128
            bk, bB = nf()
            for c in range(16):
                MM(bk[:, 0:N], wv_[:, c, 128:256], hT[:, c, 0:N], c == 0, c == 15, [wvB, hTB[c]], [bB])
            CP(ACT_, kdst[:, 0:N], bk[:, 0:N], [bB], [kdstB])
            bk, bB = nf()
            for t in range(nt):
                for c in range(16):
                    MM(bk[:, t * 128:(t + 1) * 128], hT[:, c, t * 128:(t + 1) * 128], wv_[:, c, 256:384], c == 0, c == 15,
                       [wvB, hTB[c]], [bB])
            for t in range(nt):
                CP(DVE, vdst_fn(t), bk[:, t * 128:(t + 1) * 128].rearrange("p (h d) -> p h d", h=2), [bB], [vdstB])

        def attention_pair(p, wv_, wvB, nt, tile0):
            N = nt * 128
            bk, bB = nf()
            for c in range(16):
                MM(bk[:, 0:N], wv_[:, c, 0:128], hT[:, c, 0:N], c == 0, c == 15, [wvB, hTB[c]], [bB])
            A(qT[:, 0:N], bk[:, 0:N], AF.Copy, [bB], [qTB], scale=0.125)
            attn_kv(p, wv_, wvB, nt, kTcur, kTcurB, lambda t: Vcur[:, t, :, 0:64], VcurB)
            obs = [(pf[4], pfB[4]), (pf[5], pfB[5])]
            for hh in range(2):
                k.dma(SP, biasTr[hh], biasb[hh][:], biasT_d[2 * p + hh], writes=[biasB[hh]])
            steps = []
            for j in range(4 + nt):
                i0 = max(j - 4, 0)
                i1 = min(j, nt - 1)
                if i1 < i0:
                    continue
                for hh in range(2):
                    steps.append((hh, j, i0, i1))

            def src_of(hh, j):
                ps_ = slice(64 * hh, 64 * hh + 64)
                if j < 4:
                    return (kTprev[ps_, p, j * 128:(j + 1) * 128], kTprevB[p], Vprev[:, j, 2 * p + hh, :], VprevB[p])
                return (kTcur[ps_, (j - 4) * 128:(j - 3) * 128], kTcurB, Vcur[:, j - 4, hh, :], VcurB)

            def stageA(st_):
                hh, j, i0, i1 = st_
                ps_ = slice(64 * hh, 64 * hh + 64)
                ncol = (i1 - i0 + 1) * 128
                ksrc, ksB, _, _ = src_of(hh, j)
                sb_, sB = nf()
                MM(sb_[:, 0:ncol], ksrc, qT[ps_, i0 * 128:(i1 + 1) * 128], True, True, [ksB, qTB], [sB])
                pi = cnt["pt"] % NPT
                cnt["pt"] += 1
                b0 = (i0 - j + 4) * 128
                STT(DVE, sb_[:, 0:ncol], sb_[:, 0:ncol], 70.0, biasb[hh][:, b0:b0 + ncol], ALU.min, ALU.add,
                    [sB, biasB[hh]], [sB])
                u = tile0 + j
                A(PT[pi][:, 0:ncol], sb_[:, 0:ncol], AF.Exp, [sB, constB], [PTB[pi]], bias=keymask[:, u:u + 1])
                return pi

            def stageB(st_, pi):
                hh, j, i0, i1 = st_
                _, _, vsrc, vsB = src_of(hh, j)
                ob, oB = obs[hh]
                for i in range(i0, i1 + 1):
                    MM(ob[:, i * 65:(i + 1) * 65], PT[pi][:, (i - i0) * 128:(i - i0 + 1) * 128], vsrc,
                       j == 0 and i == 0, j == 3 + nt and i == nt - 1, [PTB[pi], vsB], [oB])
            pend = []
            for st_ in steps:
                pend.append((st_, stageA(st_)))
                if len(pend) > 3:
                    s0, p0 = pend.pop(0)
                    stageB(s0, p0)
            for s0, p0 in pend:
                stageB(s0, p0)
            for hh in range(2):
                ob, oB = obs[hh]
                ov = ob[:, 0:nt * 65].rearrange("p (t e) -> p t e", e=65)
                TS(DVE, rec[:, hh, 0:nt], ov[:, :, 64], 1e-30, None, ALU.max, None, [oB], [recB[hh]])
                k.op(DVE, lambda hh=hh: nc.vector.reciprocal(out=rec[:, hh, 0:nt], in_=rec[:, hh, 0:nt]), [recB[hh]], [recB[hh]])
                for i in range(nt):
                    if hh == 0:
                        A(ypair[:, i, 0:64], ob[:, i * 65:i * 65 + 64], AF.Copy, [oB, recB[hh]], [ypairB],
                          scale=rec[:, hh, i:i + 1])
                    else:
                        TS(DVE, ypair[:, i, 64:128], ob[:, i * 65:i * 65 + 64], rec[:, hh, i:i + 1], None,
                           ALU.mult, None, [oB, recB[hh]], [ypairB])
            bk, bB = nb()
            for i in range(nt):
                TR(bk[:, i * 128:(i + 1) * 128], ypair[:, i, :], [ypairB, constB], [bB])
            CP(ACT_, yT[:, p, 0:N], bk[:, 0:N], [bB], [yTB[p]])
            shift_prev(p, nt)

        def shift_prev(p, nt):
            for jj in range(4 - nt):
                CP(POOL, kTprev[:, p, jj * 128:(jj + 1) * 128], kTprev[:, p, (jj + nt) * 128:(jj + nt + 1) * 128],
                   [kTprevB[p]], [kTprevB[p]])
                CP(POOL, Vprev[:, jj, 2 * p:2 * p + 2, :], Vprev[:, jj + nt, 2 * p:2 * p + 2, :], [VprevB[p]], [VprevB[p]])
            CP(POOL, kTprev[:, p, (4 - nt) * 128:512], kTcur[:, 0:nt * 128], [kTcurB], [kTprevB[p]])
            CP(POOL, Vprev[:, 4 - nt:4, 2 * p:2 * p + 2, :], Vcur[:, 0:nt, :, :], [VcurB], [VprevB[p]])

        def gla_head_a(h, wa, waB, nt):
            N = nt * 128
            hs = slice(h * 128, (h + 1) * 128)
            bkb, bbB = nf()
            bke, beB = nf()
            for t in range(nt):
                MM(bkb[:, t * 128:(t + 1) * 128], la[:, t, hs], triu[:, :], True, True, [laB[t], constB], [bbB])
                MM(bke[:, t * 128:(t + 1) * 128], tril[:, :], la[:, t, hs], True, True, [laB[t], constB], [beB])
            A(eb[:, 0:N], bkb[:, 0:N], AF.Exp, [bbB], [ebB])
            A(enb[:, 0:N], bkb[:, 0:N], AF.Exp, [bbB], [enbB], scale=-1.0)
            A(ee2m[:, 0:N], bke[:, 0:N], AF.Exp, [beB], [ee2mB])
            bk, bB = nf()
            for c in range(16):
                MM(bk[:, 0:N], wa[:, c, 0:128], hT[:, c, 0:N], c == 0, c == 15, [waB, hTB[c]], [bB])
            STT(DVE, qdec[:, 0:N], bk[:, 0:N], 128.0 ** -0.5, eb[:, 0:N], ALU.mult, ALU.mult, [bB, ebB], [qdecB])
            bk, bB = nf()
            for c in range(16):
                MM(bk[:, 0:N], wa[:, c, 128:256], hT[:, c, 0:N], c == 0, c == 15, [waB, hTB[c]], [bB])
            TT(DVE, kinv[:, 0:N], bk[:, 0:N], enb[:, 0:N], ALU.mult, [bB, enbB], [kinvB])
            bk, bB = nf()
            for t in range(nt):
                for c in range(16):
                    MM(bk[:, t * 128:(t + 1) * 128], hT[:, c, t * 128:(t + 1) * 128], wa[:, c, 128:256], c == 0, c == 15,
                       [waB, hTB[c]], [bB])
            TT(DVE, ktem[:, 0:nt, :], bk[:, 0:N].rearrange("p (t d) -> p t d", d=128),
               ee2m[:, 0:N].rearrange("p (t d) -> p t d", d=128), ALU.mult, [bB, ee2mB], [ktemB])
            if "f" in pend_tr:
                pend_tr.pop("f")()

        def gla_head_b(h, wb_, wbB, nt):
            N = nt * 128
            for t0 in range(0, nt, 2):
                bk, bB = nf()
                tn = min(2, nt - t0)
                for t in range(t0, t0 + tn):
                    for c in range(16):
                        MM(bk[:, (t - t0) * 256:(t - t0 + 1) * 256], hT[:, c, t * 128:(t + 1) * 128], wb_[:, c, 0:256],
                           c == 0, c == 15, [wbB, hTB[c]], [bB])
                CP(ACT_, vg[:, t0:t0 + tn, :], bk[:, 0:tn * 256].rearrange("p (t d) -> p t d", d=256), [bB], [vgB])
            for t0 in range(0, nt, 2):
                bk, bB = nf()
                tn = min(2, nt - t0)
                for t in range(t0, t0 + tn):
                    for c in range(16):
                        MM(bk[:, (t - t0) * 256:(t - t0 + 1) * 256], hT[:, c, t * 128:(t + 1) * 128], wb_[:, c, 256:512],
                           c == 0, c == 15, [wbB, hTB[c]], [bB])
                A(silr[:, t0:t0 + tn, :], bk[:, 0:tn * 256].rearrange("p (t d) -> p t d", d=256), AF.Silu, [bB], [silrB])
            ba, baB = nf()
            for t in range(nt):
                tsl = slice(t * 128, (t + 1) * 128)
                MM(ba[:, tsl], kinv[:, tsl], qdec[:, tsl], True, True, [kinvB, qdecB], [baB])
            bus = []
            for t0 in range(0, nt, 2):
                bu, buB = nf()
                for t in range(t0, min(t0 + 2, nt)):
                    MM(bu[:, (t - t0) * 256:(t - t0 + 1) * 256], ktem[:, t, :], vg[:, t, :], True, True, [ktemB, vgB], [buB])
                bus.append((bu, buB))
            for t in range(nt):
                TT(DVE, attT4[:, t, :], ba[:, t * 128:(t + 1) * 128], maskT[:], ALU.mult, [baB, constB], [attT4B[t]])
            for t in range(nt):
                CP(ACT_, Sbf4[:, t, :], S[:, h, :], [SB_], [Sbf4B[t]])
                bu, buB = bus[t // 2]
                STT(DVE, S[:, h, :], S[:, h, :], eb[:, t * 128 + 127:t * 128 + 128], bu[:, (t % 2) * 256:(t % 2 + 1) * 256],
                    ALU.mult, ALU.add, [SB_, ebB, buB], [SB_])
            bos = []
            for t0 in range(0, nt, 2):
                bo, boB = nf()
                for t in range(t0, min(t0 + 2, nt)):
                    tsl = slice(t * 128, (t + 1) * 128)
                    osl = slice((t - t0) * 256, (t - t0 + 1) * 256)
                    MM(bo[:, osl], attT4[:, t, :], vg[:, t, :], True, False, [attT4B[t], vgB], [boB])
                    MM(bo[:, osl], qdec[:, tsl], Sbf4[:, t, :], False, True, [qdecB, Sbf4B[t]], [boB])
                bos.append((bo, boB))
            for t in range(nt):
                bo, boB = bos[t // 2]
                osl = slice((t % 2) * 256, (t % 2 + 1) * 256)
                MS(DVE, ss2[:], 0.0, [ss2B])
                A(yg4[:, t, :], bo[:, osl], AF.Square, [boB], [yg4B[t], ss2B], scale=1.0 / 16.0, accum_out=ss2[:])
                RSTD(rstd2, rstd2B, ss2, ss2B)
                STT(DVE, otmp[:], bo[:, osl], rstd2[:, 0:1], gnw[:], ALU.mult, ALU.mult, [boB, rstd2B, constB], [otmpB])
                TT(DVE, yg4[:, t, :], otmp[:], silr[:, t, :], ALU.mult, [otmpB, silrB], [yg4B[t]])

            def finish_tr(h=h, nt=nt, N=N):
                ytb, ytB = nb()
                for t in range(nt):
                    for hf in range(2):
                        TR(ytb[:, hf * 512 + t * 128:hf * 512 + (t + 1) * 128], yg4[:, t, hf * 128:(hf + 1) * 128],
                           [yg4B[t], constB], [ytB])
                for hf in range(2):
                    CP(ACT_, yT[:, 8 + 2 * h + hf, 0:N], ytb[:, hf * 512:hf * 512 + N], [ytB], [yTB[8 + 2 * h + hf]])
            if h < 3:
                pend_tr["f"] = finish_tr
            else:
                finish_tr()

        widx = [i for i, it in enumerate(items) if it[0] is not None]
        loaded = {}
        nxt = 0

        def prefetch(upto):
            nonlocal nxt
            while nxt < len(widx) and nxt <= upto:
                i = widx[nxt]
                loaded[i] = load_w(items[i][0], items[i][2], items[i][3])
                nxt += 1
        wpos = 0
        for i, (parts, fn, key, first) in enumerate(items):
            if max_items is not None and i >= max_items:
                break
            if parts is None:
                fn()
            else:
                prefetch(wpos + NW - 1)
                v, vB, bi, n = loaded.pop(i)
                fn(v, vB)
                wpos += 1

        fin = Buf("fin")
        fin.r = {tr_: tr_.count for tr_ in oT}
        for tr in dbg_outs.values():
            fin.r[tr] = tr.count
        k._deps(SP, [], [fin])
        build.stats = {e.name: (e.nops, e.nwaits) for e in k.engs}
    return nc


def _host_inputs(inputs):
    f = np.float32
    x = np.asarray(inputs["x"], f).reshape(T, D)
    p = np.asarray(inputs["p"], f).reshape(T, 256)
    table = np.asarray(inputs["att_rel_bias"], f).reshape(16, 513)
    kk = np.arange(128)[:, None]
    qq = np.arange(640)[None, :]
    dist = np.clip(qq - kk, -256, 256) + 256
    kc = kk // 64
    qc = qq // 64
    valid = (qc - kc >= 0) & (qc - kc <= 8)
    biasT = np.where(valid[None], table[:, dist], f(NEG)).astype(f)
    ident = np.eye(128, dtype=f)
    ii = np.arange(128)
    maskT = (ii[None, :] >= ii[:, None]).astype(f)
    triu = np.where(ii[:, None] <= ii[None, :], f(-1.0 / 16.0), f(0.0)).astype(f)
    tril = np.where(ii[:, None] > ii[None, :], f(-1.0 / 16.0), f(0.0)).astype(f)

    def col16(v):
        return np.ascontiguousarray(np.asarray(v, f).reshape(16, 128).T)

    convw = np.ascontiguousarray(np.asarray(inputs["w_ffn_conv"], f).reshape(3, 88, 128).transpose(2, 0, 1))
    convb = np.ascontiguousarray(np.asarray(inputs["b_ffn_conv"], f).reshape(88, 128).T)
    gnw = np.ascontiguousarray(np.broadcast_to(np.asarray(inputs["gla_norm_w"], f).reshape(1, 256), (128, 256)))
    fnw = np.ascontiguousarray(np.broadcast_to(np.asarray(inputs["final_norm_w"], f).reshape(1, D), (128, D)))
    wup = np.zeros((32, 512), f)
    wup[0:16] = np.asarray(inputs["w_gla_gate_up"], f).reshape(16, 512)
    wup[16] = np.asarray(inputs["b_gla_gate"], f).reshape(512)
    shared = {
        "biasT": biasT, "ident": ident, "maskT": maskT, "triu": triu, "tril": tril,
        "nmix": col16(inputs["norm_mix_w"]), "nffn": col16(inputs["norm_ffn_w"]), "nple": col16(inputs["norm_ple_w"]),
        "convw": convw, "convb": convb, "gnw": gnw, "fnw": fnw, "wupaug": wup,
        "w_in": np.ascontiguousarray(np.asarray(inputs["w_in"], f).reshape(D, 6160)),
        "w_out": np.ascontiguousarray(np.asarray(inputs["w_out"], f).reshape(D, D)),
        "w_ffn_up": np.ascontiguousarray(np.asarray(inputs["w_ffn_up"], f).reshape(D, 2 * DFF)),
        "w_ffn_down": np.ascontiguousarray(np.asarray(inputs["w_ffn_down"], f).reshape(DFF, D)),
        "w_ple_gate": np.ascontiguousarray(np.asarray(inputs["w_ple_gate"], f).reshape(D, D)),
        "w_ple_proj": np.ascontiguousarray(np.asarray(inputs["w_ple_proj"], f).reshape(256, D)),
    }
    in_maps = []
    for c in range(NCORE):
        m0 = TOK * c - 128
        start = m0 - PREF
        xcc = np.zeros((XC, D), f)
        lo = max(0, -start)
        xcc[lo:] = x[start + lo:start + XC]
        pcc = np.zeros((MAIN, 256), f)
        lo2 = max(0, -m0)
        pcc[lo2:] = p[m0 + lo2:m0 + MAIN]
        km = np.zeros((128, 4 + MAIN_TILES), f)
        tok = (m0 - 512) + 128 * np.arange(4 + MAIN_TILES)[None, :] + np.arange(128)[:, None]
        km[tok < 0] = NEG
        d = dict(shared)
        d["xc"] = xcc
        d["pc"] = pcc
        d["keymask"] = km
        in_maps.append(d)
    return in_maps


_NC = None


def kernel(**inputs):
    global _NC
    in_maps = _host_inputs(inputs)
    if _NC is None:
        _NC = build()
    res = run_bass_kernel_spmd(_NC, in_maps, core_ids=list(range(NCORE)))
    outs = [np.asarray(res.results[c]["out"], np.float32).reshape(TOK, D) for c in range(NCORE)]
    return np.concatenate(outs, axis=0).reshape(1, T, D)
```

```python
import math
import numpy as np
from contextlib import ExitStack
import concourse.bass as bass
import concourse.mybir as mybir
from concourse.bass_utils import run_bass_kernel_spmd

F32 = mybir.dt.float32
BF16 = mybir.dt.bfloat16
AF = mybir.ActivationFunctionType
ALU = mybir.AluOpType

D = 2048
T = 16384
NCORE = 8
TOK = T // NCORE
NPRE_ST = 28
PREF = NPRE_ST * 512
MAIN_TILES = 17
MAIN = MAIN_TILES * 128
XC = PREF + MAIN
EPS = 1e-6
NEG = -30000.0
DFF = 5632
NFC = DFF // 128


class Buf:
    __slots__ = ("name", "w", "r", "psum")

    def __init__(self, name="", psum=False):
        self.name = name
        self.w = None
        self.r = {}
        self.psum = psum


class Track:
    def __init__(self, sem, name):
        self.sem = sem
        self.count = 0
        self.name = name


class Eng:
    def __init__(self, k, name, eng):
        self.name = name
        self.eng = eng
        self.track = Track(k.new_sem("e_" + name), name)
        self.waited = {}
        self.nwaits = 0
        self.nops = 0


class K:
    def __init__(self, nc, stack):
        self.nc = nc
        self.stack = stack
        self.pe = Eng(self, "pe", nc.tensor)
        self.act = Eng(self, "act", nc.scalar)
        self.dve = Eng(self, "dve", nc.vector)
        self.pool = Eng(self, "pool", nc.gpsimd)
        self.sp = Eng(self, "sp", nc.sync)
        self.engs = [self.pe, self.act, self.dve, self.pool, self.sp]

    def new_sem(self, name):
        return self.stack.enter_context(self.nc.semaphore(name))

    def dma_track(self, name):
        return Track(self.new_sem("d_" + name), name)

    def sb(self, name, shape, dtype):
        return self.stack.enter_context(self.nc.sbuf_tensor("s_" + name, list(shape), dtype))

    def ps(self, name, shape, dtype=F32):
        return self.stack.enter_context(self.nc.psum_tensor("p_" + name, list(shape), dtype))

    def _deps(self, e, reads, writes, skip_own=False):
        need = {}

        def add(tc):
            if tc is None:
                return
            t, c = tc
            if need.get(t, 0) < c:
                need[t] = c

        for b in reads:
            add(b.w)
            if b.psum:
                for t, c in b.r.items():
                    if t is not e.track:
                        add((t, c))
        for b in writes:
            add(b.w)
            for t, c in b.r.items():
                add((t, c))
        for t, c in need.items():
            if skip_own and t is e.track:
                continue
            if e.waited.get(t, 0) >= c:
                continue
            e.eng.wait_ge(t.sem, c)
            e.waited[t] = c
            e.nwaits += 1

    def op(self, e, fn, reads=(), writes=(), pe_acc=False):
        self._deps(e, reads, writes, skip_own=(pe_acc or e is self.pe))
        ins = fn()
        t = e.track
        t.count += 1
        ins.then_inc(t.sem, 1)
        e.nops += 1
        for b in reads:
            b.r[t] = t.count
        for b in writes:
            b.w = (t, t.count)
            b.r = {}
        return ins

    def dma(self, e, track, out, in_, reads=(), writes=()):
        self._deps(e, reads, writes)
        ins = e.eng.dma_start(out=out, in_=in_)
        track.count += 16
        ins.then_inc(track.sem, 16)
        for b in reads:
            b.r[track] = track.count
        for b in writes:
            b.w = (track, track.count)
            b.r = {}
        return ins


class _Stop(Exception):
    pass


def build(dbg=None, npre_run=None, max_items=None, stop=None):
    nc = bass.Bass("TRN2", target_bir_lowering=False)

    def din(name, shape):
        return nc.dram_tensor(name, list(shape), F32, kind="ExternalInput").ap()

    xc = din("xc", [XC, D])
    pc = din("pc", [MAIN, 256])
    keymask_d = din("keymask", [128, 4 + MAIN_TILES])
    biasT_d = din("biasT", [16, 128, 640])
    ident_d = din("ident", [128, 128])
    maskT_d = din("maskT", [128, 128])
    triu_d = din("triu", [128, 128])
    tril_d = din("tril", [128, 128])
    nmix_d = din("nmix", [128, 16])
    nffn_d = din("nffn", [128, 16])
    nple_d = din("nple", [128, 16])
    convw_d = din("convw", [128, 3, 88])
    convb_d = din("convb", [128, 88])
    gnw_d = din("gnw", [128, 256])
    fnw_d = din("fnw", [128, D])
    wup_d = din("wupaug", [32, 512])
    w_in = din("w_in", [D, 6160]).rearrange("(c p) n -> p c n", p=128)
    w_out = din("w_out", [D, D]).rearrange("(c p) n -> p c n", p=128)
    w_up = din("w_ffn_up", [D, 2 * DFF]).rearrange("(c p) n -> p c n", p=128)
    w_down = din("w_ffn_down", [DFF, D]).rearrange("(c p) n -> p c n", p=128)
    w_pg = din("w_ple_gate", [D, D]).rearrange("(c p) n -> p c n", p=128)
    w_pp = din("w_ple_proj", [256, D]).rearrange("(c p) n -> p c n", p=128)
    out_d = nc.dram_tensor("out", [TOK, D], F32, kind="ExternalOutput").ap()
    dbg_outs = {}

    with ExitStack() as st:
        k = K(nc, st)
        PE, ACT_, DVE, POOL, SP = k.pe, k.act, k.dve, k.pool, k.sp

        def ckg(n):
            if stop == n:
                raise _Stop()

        def A(out, in_, func, r, w, **kw):
            return k.op(ACT_, lambda: nc.scalar.activation(out=out, in_=in_, func=func, **kw), r, w)

        def veng(e):
            return nc.vector if e is DVE else nc.gpsimd

        def TS(e, out, in0, s1, s2, op0, op1, r, w):
            if s2 is None:
                return k.op(e, lambda: veng(e).tensor_scalar(out=out, in0=in0, scalar1=s1, scalar2=None, op0=op0), r, w)
            return k.op(e, lambda: veng(e).tensor_scalar(out=out, in0=in0, scalar1=s1, scalar2=s2, op0=op0, op1=op1), r, w)

        def TT(e, out, in0, in1, op, r, w):
            return k.op(e, lambda: veng(e).tensor_tensor(out=out, in0=in0, in1=in1, op=op), r, w)

        def STT(e, out, in0, scalar, in1, op0, op1, r, w):
            return k.op(e, lambda: veng(e).scalar_tensor_tensor(out=out, in0=in0, scalar=scalar, in1=in1, op0=op0, op1=op1), r, w)

        def CP(e, out, in_, r, w):
            if e is ACT_:
                return k.op(e, lambda: nc.scalar.copy(out=out, in_=in_), r, w)
            return k.op(e, lambda: veng(e).tensor_copy(out=out, in_=in_), r, w)

        def MS(e, ap, val, w):
            return k.op(e, lambda: veng(e).memset(ap, val), (), w)

        def MM(out, lhsT, rhs, start, stop, r, w):
            return k.op(PE, lambda: nc.tensor.matmul(out, lhsT=lhsT, rhs=rhs, start=start, stop=stop), r, w,
                        pe_acc=not start)

        def RSTD(dst, dstB, src, srcB):
            ckg(10)
            TS(DVE, dst[:], src[:], EPS, None, ALU.add, None, [srcB], [dstB])
            ckg(11)
            A(dst[:], dst[:], AF.Ln, [dstB], [dstB])
            ckg(12)
            A(dst[:], dst[:], AF.Exp, [dstB], [dstB], scale=-0.5)
            ckg(13)

        def TR(out, in_, r, w):
            return k.op(PE, lambda: nc.tensor.transpose(out, in_, ident[:]), list(r) + [constSB], w)

        pf = [k.ps(f"pf{i}", [128, 512], F32) for i in range(6)]
        pfB = [Buf(f"pf{i}", psum=True) for i in range(6)]
        pb = [k.ps(f"pb{i}", [128, 1024], BF16) for i in range(2)]
        pbB = [Buf(f"pb{i}", psum=True) for i in range(2)]
        cnt = {"f": 0, "b": 0, "w": 0, "pt": 0}

        def nf():
            i = cnt["f"] % 4
            cnt["f"] += 1
            return pf[i], pfB[i]

        def nb():
            i = cnt["b"] % 2
            cnt["b"] += 1
            return pb[i], pbB[i]

        constB = Buf("const")
        tconst = k.dma_track("const")
        ident = k.sb("ident", [128, 128], BF16)
        maskT = k.sb("maskT", [128, 128], F32)
        triu = k.sb("triu", [128, 128], F32)
        tril = k.sb("tril", [128, 128], F32)
        nmix = k.sb("nmix", [128, 16], F32)
        nffn = k.sb("nffn", [128, 16], F32)
        nple = k.sb("nple", [128, 16], F32)
        convw = k.sb("convw", [128, 3, 88], F32)
        convb = k.sb("convb", [128, 88], F32)
        gnw = k.sb("gnw", [128, 256], F32)
        keymask = k.sb("keymask", [128, 4 + MAIN_TILES], F32)
        wupaug = k.sb("wupaug", [32, 512], BF16)
        constSB = Buf("constS")
        tconsts = k.dma_track("consts")
        k.dma(POOL, tconsts, ident[:], ident_d, writes=[constSB])
        k.dma(POOL, tconsts, wupaug[:], wup_d, writes=[constSB])
        for dst, src in [(maskT, maskT_d), (triu, triu_d), (tril, tril_d), (nmix, nmix_d), (nffn, nffn_d),
                         (nple, nple_d), (convw, convw_d), (convb, convb_d), (gnw, gnw_d), (keymask, keymask_d)]:
            k.dma(SP, tconst, dst[:], src, writes=[constB])

        S = k.sb("S", [128, 4, 256], F32)
        SB_ = Buf("S")
        MS(DVE, S[:], 0.0, [SB_])
        hT = k.sb("hT", [128, 16, 512], BF16)
        hTB = [Buf(f"hT{c}") for c in range(16)]
        htmp = k.sb("htmp", [128, D], BF16)
        htmpB = Buf("htmp")
        ss = k.sb("ss", [128, 1], F32)
        ssB = Buf("ss")
        rstd = k.sb("rstd", [128, 1], F32)
        rstdB = Buf("rstd")
        glT = k.sb("glT", [32, 512], BF16)
        glTB = Buf("glT")
        MS(DVE, glT[:], 1.0, [glTB])
        la = k.sb("la", [128, 4, 512], F32)
        laB = [Buf(f"la{t}") for t in range(4)]
        gtmp = k.sb("gtmp", [128, 512], F32)
        gtmpB = Buf("gtmp")

        def dbg_dump(name, ap, shape, rbufs, dtype=F32):
            if dbg is None or name not in dbg:
                return
            d = nc.dram_tensor("dbg_" + name, list(shape), dtype, kind="ExternalOutput").ap()
            tr = k.dma_track("dbg_" + name)
            k.dma(SP, tr, d, ap, reads=rbufs)
            dbg_outs[name] = tr

        class NS:
            pass
        M = NS()
        M.hT, M.hTB, M.htmp, M.htmpB, M.ss, M.ssB, M.rstd, M.rstdB = hT, hTB, htmp, htmpB, ss, ssB, rstd, rstdB
        M.la, M.laB, M.glT, M.glTB, M.gtmp, M.gtmpB = la, laB, glT, glTB, gtmp, gtmpB
        M.htmpL, M.htmpLB = [htmp], [htmpB]

        def norm_partA(xa, xB, ns, hi=0):
            htmp_, htmpB_ = ns.htmpL[hi], ns.htmpLB[hi]
            MS(DVE, ns.ss[:], 0.0, [ns.ssB])
            A(htmp_[:], xa, AF.Square, [xB], [htmpB_, ns.ssB], scale=1.0 / math.sqrt(D), accum_out=ns.ss[:])
            RSTD(ns.rstd, ns.rstdB, ns.ss, ns.ssB)
            TS(DVE, htmp_[:], xa, ns.rstd[:, 0:1], None, ALU.mult, None, [xB, ns.rstdB], [htmpB_])

        def norm_partB(t, wcol, ns, hi=0):
            htmp_, htmpB_ = ns.htmpL[hi], ns.htmpLB[hi]
            for half in range(2):
                bk, bB = nb()
                for j in range(8):
                    c = half * 8 + j
                    TR(bk[:, j * 128:(j + 1) * 128], htmp_[:, c * 128:(c + 1) * 128], [htmpB_, constB], [bB])
                for j in range(8):
                    c = half * 8 + j
                    dst = ns.hT[:, c, t * 128:(t + 1) * 128]
                    src = bk[:, j * 128:(j + 1) * 128]
                    if half == 0:
                        A(dst, src, AF.Copy, [bB, constB], [ns.hTB[c]], scale=wcol[:, c:c + 1])
                    else:
                        TS(DVE, dst, src, wcol[:, c:c + 1], None, ALU.mult, None, [bB, constB], [ns.hTB[c]])

        def norm_tile(xa, xB, t, wcol, ns):
            norm_partA(xa, xB, ns)
            norm_partB(t, wcol, ns)

        def norm_hT(xs, wcol, nt, ns=None):
            ns = ns or M
            junk = gact[:, 0:4, :].rearrange("p a b -> p (a b)")
            junkB = gactB[0:4]

            def a1(t):
                xa, xB = xs[t]
                MS(DVE, ns.ss[:], 0.0, [ns.ssB])
                A(junk, xa, AF.Square, [xB], junkB + [ns.ssB], scale=1.0 / math.sqrt(D), accum_out=ns.ss[:])
                RSTD(ns.rstd, ns.rstdB, ns.ss, ns.ssB)

            def a2(t):
                xa, xB = xs[t]
                TS(DVE, ns.htmp[:], xa, ns.rstd[:, 0:1], None, ALU.mult, None, [xB, ns.rstdB], [ns.htmpB])
            a1(0)
            a2(0)
            for t in range(nt):
                if t + 1 < nt:
                    a1(t + 1)
                norm_partB(t, wcol, ns)
                if t + 1 < nt:
                    a2(t + 1)

        def gate_g1(wg, wgB, nt, ns):
            N = nt * 128
            bk, bB = nf()
            for c in range(16):
                MM(bk[0:16, 0:N], wg[:, c, 0:16], ns.hT[:, c, 0:N], c == 0, c == 15, [wgB, ns.hTB[c]], [bB])
            CP(ACT_, ns.glT[0:16, 0:N], bk[0:16, 0:N], [bB], [ns.glTB])

        def gate_g2(nt, ns):
            for t in range(nt):
                bk, bB = nf()
                MM(bk[:, :], ns.glT[0:32, t * 128:(t + 1) * 128], wupaug[0:32, :], True, True, [ns.glTB, constSB], [bB])
                A(ns.gtmp[:], bk[:, :], AF.Exp, [bB], [ns.gtmpB], scale=-1.0)
                A(ns.la[:, t, :], ns.gtmp[:], AF.Ln, [ns.gtmpB], [ns.laB[t]], bias=1.0)

        def gate_chain(wg, wgB, nt, ns=None):
            ns = ns or M
            gate_g1(wg, wgB, nt, ns)
            gate_g2(nt, ns)

        HALF = NFC // 2
        scr = {}
        items = []

        seqc = {"n": 0, "first": True, "use": False}

        def add(parts, fn, skip=False):
            if skip:
                if parts is not None and seqc["use"]:
                    seqc["n"] += 1
                return
            if parts is None or not seqc["use"]:
                items.append((parts, fn, None, True))
            else:
                items.append((parts, fn, seqc["n"], seqc["first"]))
                seqc["n"] += 1

        def main_st(tile0, nt, store):
            N = nt * 128
            r0 = PREF + tile0 * 128
            seqc["n"] = 0
            seqc["first"] = (tile0 == 1)
            seqc["use"] = True

            def stage_load():
                for t in range(nt):
                    k.dma(SP, xT_[t], xres[t][:], xc[r0 + t * 128:r0 + (t + 1) * 128, :], writes=[xresB[t]])
                norm_hT([(xres[t][:], xresB[t]) for t in range(nt)], nmix, nt)
            add(None, stage_load)
            for p in range(8):
                add([w_in[:, :, 128 * p:128 * p + 128], w_in[:, :, 1024 + 128 * p:1024 + 128 * p + 128],
                     w_in[:, :, 2048 + 128 * p:2048 + 128 * p + 128]],
                    lambda v, vB, p=p: attention_pair(p, wjoin(v), vB, nt, tile0))
            add([w_in[:, :, 6144:6160]], lambda v, vB: gate_chain(v[0], vB, nt))
            for h in range(4):
                add([w_in[:, :, 3072 + 128 * h:3072 + 128 * h + 128], w_in[:, :, 3584 + 128 * h:3584 + 128 * h + 128]],
                    lambda v, vB, h=h: gla_head_a(h, wjoin(v), vB, nt))
                add([w_in[:, :, 4096 + 256 * h:4096 + 256 * h + 256], w_in[:, :, 5120 + 256 * h:5120 + 256 * h + 256]],
                    lambda v, vB, h=h: gla_head_b(h, wjoin(v), vB, nt))
            for g in range(4):
                def f_out(v, vB, g=g):
                    for t in range(nt):
                        bk, bB = nf()
                        for c in range(16):
                            MM(bk[:, :], yT[:, c, t * 128:(t + 1) * 128], v[0][:, c, :], c == 0, c == 15, [yTB[c], vB], [bB])
                        TT(DVE, xres[t][:, g * 512:(g + 1) * 512], xres[t][:, g * 512:(g + 1) * 512], bk[:, :], ALU.add,
                           [xresB[t], bB], [xresB[t]])
                add([w_out[:, :, g * 512:(g + 1) * 512]], f_out)
            add(None, lambda: norm_hT([(xres[t][:], xresB[t]) for t in range(nt)], nffn, nt))
            for half in range(2):
                for blk in range(HALF // 2):
                    def f_up(v, vB, half=half, blk=blk):
                        for jj in range(2):
                            jq = blk * 2 + jj
                            j = half * HALF + jq
                            for br in range(2):
                                ch = j + br * NFC
                                bk, bB = nf()
                                for c in range(16):
                                    MM(bk[:, 0:N], v[br][:, c, jj * 128:(jj + 1) * 128], hT[:, c, 0:N], c == 0, c == 15,
                                       [vB, hTB[c]], [bB])
                                CP(POOL, ext[br][:, 0:2], tail[:, ch, :], [tailB], [extB[br]])
                                CP(ACT_, ext[br][:, 2:2 + N], bk[:, 0:N], [bB], [extB[br]])
                                CP(POOL, tail[:, ch, :], ext[br][:, N:N + 2], [extB[br]], [tailB])
                                TS(DVE, ycv[br][:, 0:N], ext[br][:, 2:2 + N], convw[:, 2, ch:ch + 1], convb[:, ch:ch + 1],
                                   ALU.mult, ALU.add, [extB[br], constB], [ycvB[br]])
                                STT(DVE, ycv[br][:, 0:N], ext[br][:, 1:1 + N], convw[:, 1, ch:ch + 1], ycv[br][:, 0:N],
                                    ALU.mult, ALU.add, [extB[br], constB, ycvB[br]], [ycvB[br]])
                                STT(DVE, ycv[br][:, 0:N], ext[br][:, 0:N], convw[:, 0, ch:ch + 1], ycv[br][:, 0:N],
                                    ALU.mult, ALU.add, [extB[br], constB, ycvB[br]], [ycvB[br]])
                            A(ycv[0][:, 0:N], ycv[0][:, 0:N], AF.Gelu, [ycvB[0]], [ycvB[0]])
                            TT(DVE, gact[:, jq, 0:N], ycv[0][:, 0:N], ycv[1][:, 0:N], ALU.mult, [ycvB[0], ycvB[1]],
                               [gactB[jq]])
                    c0 = (half * HALF + blk * 2) * 128
                    add([w_up[:, :, c0:c0 + 256], w_up[:, :, DFF + c0:DFF + c0 + 256]], f_up)
                banks = {}
                for g in range(4):
                    for kg in range(2):
                        def f_dn(v, vB, g=g, kg=kg, half=half):
                            for t in range(nt):
                                if kg == 0:
                                    banks[(g, t)] = nf()
                                bk, bB = banks[(g, t)]
                                for jj in range(11):
                                    jq = kg * 11 + jj
                                    MM(bk[:, :], gact[:, jq, t * 128:(t + 1) * 128], v[0][:, jj, :],
                                       kg == 0 and jj == 0, kg == 1 and jj == 10, [gactB[jq], vB], [bB])
                                if kg == 1:
                                    TT(DVE, xres[t][:, g * 512:(g + 1) * 512], xres[t][:, g * 512:(g + 1) * 512], bk[:, :],
                                       ALU.add, [xresB[t], bB], [xresB[t]])
                        f0 = half * HALF + kg * 11
                        add([w_down[:, f0:f0 + 11, g * 512:(g + 1) * 512]], f_dn, skip=not store)
            def stage_ple():
                norm_hT([(xres[t][:], xresB[t]) for t in range(nt)], nple, nt)
                bk, bB = nb()
                for t in range(nt):
                    pr0 = tile0 * 128 + t * 128
                    k.dma(SP, pinT, pin[:], pc[pr0:pr0 + 128, :], writes=[pinB])
                    CP(DVE, pbf[:], pin[:], [pinB], [pbfB])
                    for kc in range(2):
                        TR(bk[:, kc * 512 + t * 128:kc * 512 + (t + 1) * 128], pbf[:, kc * 128:(kc + 1) * 128],
                           [pbfB, constB], [bB])
                for kc in range(2):
                    CP(ACT_, pT[:, kc, 0:N], bk[:, kc * 512:kc * 512 + N], [bB], [pTB])
            add(None, stage_ple, skip=not store)
            for g in range(8):
                def f_ple(v, vB, g=g):
                    for t in range(nt):
                        bk, bB = nf()
                        for c in range(16):
                            MM(bk[:, 0:256], hT[:, c, t * 128:(t + 1) * 128], v[0][:, c, :], c == 0, c == 15, [hTB[c], vB], [bB])
                        A(gsig[:, 0:256], bk[:, 0:256], AF.Sigmoid, [bB], [gsigB])
                        b2, b2B = nf()
                        for kc in range(2):
                            MM(b2[:, 0:256], pT[:, kc, t * 128:(t + 1) * 128], v[1][:, kc, :], kc == 0, kc == 1, [pTB, vB], [b2B])
                        TT(DVE, gsig[:, 0:256], gsig[:, 0:256], b2[:, 0:256], ALU.mult, [gsigB, b2B], [gsigB])
                        TT(DVE, xres[t][:, g * 256:(g + 1) * 256], xres[t][:, g * 256:(g + 1) * 256], gsig[:, 0:256], ALU.add,
                           [xresB[t], gsigB], [xresB[t]])
                add([w_pg[:, :, g * 256:(g + 1) * 256], w_pp[:, :, g * 256:(g + 1) * 256]], f_ple, skip=not store)
            def stage_final():
                if not store:
                    return
                for t in range(nt):
                    MS(DVE, ss[:], 0.0, [ssB])
                    A(htmp[:], xres[t][:], AF.Square, [xresB[t]], [htmpB, ssB], scale=1.0 / math.sqrt(D), accum_out=ss[:])
                    RSTD(rstd, rstdB, ss, ssB)
                    STT(DVE, xres[t][:], xres[t][:], rstd[:, 0:1], fnw[:], ALU.mult, ALU.mult, [xresB[t], rstdB, fnwB],
                        [xresB[t]])
                    o0 = (tile0 - 1 + t) * 128
                    k.dma(SP, oT[t], out_d[o0:o0 + 128, :], xres[t][:], reads=[xresB[t]])
            add(None, stage_final, skip=not store)

        def wjoin(v):
            return WJ(v)

        class WJ:
            def __init__(self, parts):
                self.parts = parts
                self.offs = []
                o = 0
                for pp in parts:
                    self.offs.append(o)
                    o += pp.shape[2]
                self.n = o

            def __getitem__(self, key):
                ps_, c, cols = key
                a, b = cols.start, cols.stop
                for pp, o in zip(self.parts, self.offs):
                    if a >= o and b <= o + pp.shape[2]:
                        return pp[ps_, c, a - o:b - o]
                raise IndexError((a, b))

        def stage_halo():
            for t in range(4):
                r0 = PREF - 512 + t * 128
                k.dma(SP, xT_[t], xres[t][:], xc[r0:r0 + 128, :], writes=[xresB[t]])
            norm_hT([(xres[t][:], xresB[t]) for t in range(4)], nmix, 4)
        add(None, stage_halo)
        for p in range(8):
            def f_halo(v, vB, p=p):
                wv_ = WJ([v[0], v[0], v[1]])
                attn_kv(p, wv_, vB, 4, kTprev[:, p, :], kTprevB[p], lambda t: Vprev[:, t, 2 * p:2 * p + 2, 0:64], VprevB[p])
            add([w_in[:, :, 1024 + 128 * p:1024 + 128 * p + 128], w_in[:, :, 2048 + 128 * p:2048 + 128 * p + 128]], f_halo)
        main_st(0, 1, False)
        for s_ in range(4):
            main_st(1 + 4 * s_, 4, True)

        with ExitStack() as pst:
            def psb(name, shape, dtype):
                return pst.enter_context(nc.sbuf_tensor("s_" + name, list(shape), dtype))

            wpre = psb("wpre", [128, 16, 1552], BF16)
            wpreB = Buf("wpre")
            twp = k.dma_track("wpre")
            k.dma(POOL, twp, wpre[:, :, 0:512], w_in[:, :, 3584:4096], writes=[wpreB])
            k.dma(POOL, twp, wpre[:, :, 512:1024], w_in[:, :, 4096:4608], writes=[wpreB])
            k.dma(POOL, twp, wpre[:, :, 1024:1536], w_in[:, :, 4608:5120], writes=[wpreB])
            k.dma(POOL, twp, wpre[:, :, 1536:1552], w_in[:, :, 6144:6160], writes=[wpreB])
            NBG = 8
            tbg = [k.dma_track(f"bg{i}") for i in range(NBG)]
            for (parts_, fn_, key_, first_) in items:
                if key_ is None or not first_:
                    continue
                n_ = sum(pp.shape[1] * pp.shape[2] for pp in parts_)
                d_ = nc.dram_tensor(f"scr{key_}", [128, n_], BF16).ap()
                off_ = 0
                for src_ in parts_:
                    nch_, ncol_ = src_.shape[1], src_.shape[2]
                    k.dma(POOL, tbg[key_ % NBG], d_[:, off_:off_ + nch_ * ncol_].rearrange("p (c n) -> p c n", c=nch_), src_)
                    off_ += nch_ * ncol_
                scr[key_] = d_
            NXP = 4
            xp = [psb(f"xp{i}", [128, D], F32) for i in range(NXP)]
            xpB = [Buf(f"xp{i}") for i in range(NXP)]
            xpT = [k.dma_track(f"xp{i}") for i in range(NXP)]
            PS = []
            for par in range(2):
                ns = NS()
                if par == 0:
                    ns.__dict__.update(M.__dict__)
                    ns.htmpL = [htmp, psb("htmp_a1", [128, D], BF16)]
                    ns.htmpLB = [htmpB, Buf("htmp_a1")]
                else:
                    ns.hT = psb("hT_b", [128, 16, 512], BF16)
                    ns.hTB = [Buf(f"hTb{c}") for c in range(16)]
                    ns.htmpL = [psb(f"htmp_b{i}", [128, D], BF16) for i in range(2)]
                    ns.htmpLB = [Buf(f"htmp_b{i}") for i in range(2)]
                    ns.ss = psb("ss_b", [128, 1], F32)
                    ns.ssB = Buf("ss_b")
                    ns.rstd = psb("rstd_b", [128, 1], F32)
                    ns.rstdB = Buf("rstd_b")
                    ns.la = psb("la_b", [128, 4, 512], F32)
                    ns.laB = [Buf(f"la_b{t}") for t in range(4)]
                    ns.glT = psb("glT_b", [32, 512], BF16)
                    ns.glTB = Buf("glT_b")
                    MS(DVE, ns.glT[:], 1.0, [ns.glTB])
                    ns.gtmp = psb("gtmp_b", [128, 512], F32)
                    ns.gtmpB = Buf("gtmp_b")
                ns.ee2 = [psb(f"ee2_{par}{i}", [128, 512], F32) for i in range(2)]
                ns.ee2B = [Buf(f"ee2_{par}{i}") for i in range(2)]
                ns.decp = [psb(f"decp_{par}{i}", [128, 4], F32) for i in range(2)]
                ns.decpB = [Buf(f"decp_{par}{i}") for i in range(2)]
                ns.kte = [psb(f"kte_{par}{i}", [128, 512], BF16) for i in range(2)]
                ns.kteB = [Buf(f"kte_{par}{i}") for i in range(2)]
                ns.vpre = [psb(f"vpre_{par}{i}", [128, 1024], BF16) for i in range(2)]
                ns.vpreB = [Buf(f"vpre_{par}{i}") for i in range(2)]
                PS.append(ns)
            xcnt = 0
            wgp = wpre[:, :, 1536:1552]

            def ck(n):
                if stop == n:
                    raise _Stop()

            def pA(sti, t, ns, hi):
                nonlocal xcnt
                i = xcnt % NXP
                xcnt += 1
                r0 = sti * 512 + t * 128
                k.dma(SP, xpT[i], xp[i][:], xc[r0:r0 + 128, :], writes=[xpB[i]])
                norm_partA(xp[i][:], xpB[i], ns, hi)

            def mm_tile(t, ns):
                tsl = slice(t * 128, (t + 1) * 128)
                q = t % 2
                bk, bB = nf()
                MM(bk[:, :], tril[:, :], ns.la[:, t, :], True, True, [constB, ns.laB[t]], [bB])
                A(ns.ee2[q][:], bk[:, :], AF.Exp, [bB], [ns.ee2B[q]])
                bk, bB = nf()
                for h in range(4):
                    MM(bk[:, h:h + 1], ns.la[:, t, h * 128:(h + 1) * 128], triu[:, 127:128], True, True,
                       [constB, ns.laB[t]], [bB])
                A(ns.decp[q][:], bk[:, 0:4], AF.Exp, [bB], [ns.decpB[q]])
                bk, bB = nf()
                for c in range(16):
                    MM(bk[:, :], ns.hT[:, c, tsl], wpre[:, c, 0:512], c == 0, c == 15, [ns.hTB[c], wpreB], [bB])
                TT(DVE, ns.kte[q][:], bk[:, :], ns.ee2[q][:], ALU.mult, [bB, ns.ee2B[q]], [ns.kteB[q]])
                for vh in range(2):
                    bk, bB = nf()
                    for c in range(16):
                        MM(bk[:, :], ns.hT[:, c, tsl], wpre[:, c, 512 + vh * 512:1024 + vh * 512], c == 0, c == 15,
                           [ns.hTB[c], wpreB], [bB])
                    CP(ACT_, ns.vpre[q][:, vh * 512:(vh + 1) * 512], bk[:, :], [bB], [ns.vpreB[q]])
                for h in range(4):
                    bk, bB = nf()
                    MM(bk[:, 0:256], ns.kte[q][:, h * 128:(h + 1) * 128], ns.vpre[q][:, h * 256:(h + 1) * 256], True, True,
                       [ns.kteB[q], ns.vpreB[q]], [bB])
                    STT(DVE, S[:, h, :], S[:, h, :], ns.decp[q][:, h:h + 1], bk[:, 0:256], ALU.mult, ALU.add,
                        [SB_, ns.decpB[q], bB], [SB_])
            try:
                sts = list(range(NPRE_ST - (NPRE_ST if npre_run is None else npre_run), NPRE_ST))
                if sts:
                    ns0 = PS[0]
                    for t in range(4):
                        pA(sts[0], t, ns0, t % 2)
                        norm_partB(t, nmix, ns0, t % 2)
                    gate_chain(wgp, wpreB, 4, ns0)
                sched = {0: [(0, 0), (1, 1)], 1: [(2, 0)], 2: [(3, 1)], 3: []}
                for si, sti in enumerate(sts):
                    ns = PS[si % 2]
                    nn = PS[(si + 1) % 2]
                    nxt_ = sts[si + 1] if si + 1 < len(sts) else None
                    for t in range(4):
                        if nxt_ is not None:
                            for (tt, hi) in sched[t]:
                                pA(nxt_, tt, nn, hi)
                            if t == 3:
                                gate_g1(wgp, wpreB, 4, nn)
                        mm_tile(t, ns)
                        if nxt_ is not None:
                            for (tt, hi) in sched[t]:
                                norm_partB(tt, nmix, nn, hi)
                            if t == 3:
                                gate_g2(4, nn)
            except _Stop:
                pass
            for e in k.engs:
                for o in k.engs:
                    if o.track.count > e.waited.get(o.track, 0):
                        e.eng.wait_ge(o.track.sem, o.track.count)
                        e.waited[o.track] = o.track.count
                for tr in xpT + [twp] + tbg:
                    if tr.count > e.waited.get(tr, 0):
                        e.eng.wait_ge(tr.sem, tr.count)
                        e.waited[tr] = tr.count
        dbg_dump("S_in", S[:], [128, 4, 256], [SB_])

        NW = 2
        WSZ = 8192
        wbuf = [k.sb(f"wbuf{i}", [128, WSZ], BF16) for i in range(NW)]
        wB = [Buf(f"wbuf{i}") for i in range(NW)]
        wT = [k.dma_track(f"w{i}") for i in range(NW)]

        wT2 = [k.dma_track(f"wh{i}") for i in range(NW)]

        def load_w(parts, key=None, first=True):
            i = cnt["w"] % NW
            cnt["w"] += 1
            off = 0
            views = []
            for src in parts:
                nch, ncol = src.shape[1], src.shape[2]
                views.append(wbuf[i][:, off:off + nch * ncol].rearrange("p (c n) -> p c n", c=nch))
                off += nch * ncol
            assert off <= WSZ
            if key is None:
                for src, dst in zip(parts, views):
                    k.dma(POOL, wT[i], dst, src, writes=[wB[i]])
            else:
                k.dma(SP, wT2[i], wbuf[i][:, 0:off], scr[key], writes=[wB[i]])
            return views, wB[i], i, off

        xres = [k.sb(f"xres{t}", [128, D], F32) for t in range(4)]
        xresB = [Buf(f"xres{t}") for t in range(4)]
        xT_ = [k.dma_track(f"xres{t}") for t in range(4)]
        yT = k.sb("yT", [128, 16, 512], BF16)
        yTB = [Buf(f"yT{c}") for c in range(16)]
        kTprev = k.sb("kTprev", [128, 8, 512], BF16)
        kTprevB = [Buf(f"kTprev{p}") for p in range(8)]
        Vprev = k.sb("Vprev", [128, 4, 16, 65], BF16)
        VprevB = [Buf(f"Vprev{p}") for p in range(8)]
        MS(DVE, Vprev[:], 1.0, VprevB)
        qT = k.sb("qT", [128, 512], BF16)
        qTB = Buf("qT")
        kTcur = k.sb("kTcur", [128, 512], BF16)
        kTcurB = Buf("kTcur")
        Vcur = k.sb("Vcur", [128, 4, 2, 65], BF16)
        VcurB = Buf("Vcur")
        MS(DVE, Vcur[:], 1.0, [VcurB])
        biasb = [k.sb(f"biasb{i}", [128, 640], F32) for i in range(2)]
        biasB = [Buf(f"biasb{i}") for i in range(2)]
        biasTr = [k.dma_track(f"bias{i}") for i in range(2)]
        NPT = 4
        PT = [k.sb(f"PT{i}", [128, 512], BF16) for i in range(NPT)]
        PTB = [Buf(f"PT{i}") for i in range(NPT)]
        rec = k.sb("rec", [128, 2, 4], F32)
        recB = [Buf("rec0"), Buf("rec1")]
        eb = k.sb("eb", [128, 512], F32)
        ebB = Buf("eb")
        enb = k.sb("enb", [128, 512], F32)
        enbB = Buf("enb")
        ee2m = k.sb("ee2m", [128, 512], F32)
        ee2mB = Buf("ee2m")
        qdec = k.sb("qdec", [128, 512], BF16)
        qdecB = Buf("qdec")
        ypair = qdec[:].rearrange("p (t d) -> p t d", d=128)
        ypairB = qdecB
        kinv = kTcur
        kinvB = kTcurB
        ktem = k.sb("ktem", [128, 4, 128], BF16)
        ktemB = Buf("ktem")
        vg = k.sb("vg", [128, 4, 256], BF16)
        vgB = Buf("vg")
        silr = k.sb("silr", [128, 4, 256], F32)
        silrB = Buf("silr")
        Sbf4 = k.sb("Sbf4", [128, 4, 256], BF16)
        Sbf4B = [Buf(f"Sbf4_{t}") for t in range(4)]
        attT4 = k.sb("attT4", [128, 4, 128], BF16)
        attT4B = [Buf(f"attT4_{t}") for t in range(4)]
        otmp = k.sb("otmp", [128, 256], F32)
        otmpB = Buf("otmp")
        yg4 = k.sb("yg4", [128, 4, 256], BF16)
        yg4B = [Buf(f"yg4_{t}") for t in range(4)]
        yg = yg4[:, 0, :]
        ygB = yg4B[0]
        pend_tr = {}
        pend_att = {}
        ss2 = k.sb("ss2", [128, 1], F32)
        ss2B = Buf("ss2")
        rstd2 = k.sb("rstd2", [128, 1], F32)
        rstd2B = Buf("rstd2")
        HALF = NFC // 2
        gact = k.sb("gact", [128, HALF, 512], BF16)
        gactB = [Buf(f"gact{j}") for j in range(HALF)]
        ext = [k.sb(f"ext{i}", [128, 516], F32) for i in range(2)]
        extB = [Buf(f"ext{i}") for i in range(2)]
        ycv = [gtmp, k.sb("ycv1", [128, 512], F32)]
        ycvB = [gtmpB, Buf("ycv1")]
        tail = k.sb("tail", [128, 88, 2], F32)
        tailB = Buf("tail")
        MS(DVE, tail[:], 0.0, [tailB])
        pin = otmp
        pinB = otmpB
        pinT = k.dma_track("pin")
        pbf = yg
        pbfB = ygB
        pT = k.sb("pT", [128, 2, 512], BF16)
        pTB = Buf("pT")
        gsig = ycv[1]
        gsigB = ycvB[1]
        fnw = k.sb("fnw", [128, D], F32)
        fnwB = Buf("fnw")
        k.dma(SP, tconst, fnw[:], fnw_d, writes=[fnwB])
        oT = [k.dma_track(f"out{t}") for t in range(4)]

        def attn_kv(p, wv_, wvB, nt, kdst, kdstB, vdst_fn, vdstB):
            N = nt * 128
            bk, bB = nf()
            for c in range(16):
                MM(bk[:, 0:N], wv_[:, c, 128:256], hT[:, c, 0:N], c == 0, c == 15, [wvB, hTB[c]], [bB])
            CP(ACT_, kdst[:, 0:N], bk[:, 0:N], [bB], [kdstB])
            bk, bB = nf()
            for t in range(nt):
                for c in range(16):
                    MM(bk[:, t * 128:(t + 1) * 128], hT[:, c, t * 128:(t + 1) * 128], wv_[:, c, 256:384], c == 0, c == 15,
                       [wvB, hTB[c]], [bB])
            for t in range(nt):
                CP(DVE, vdst_fn(t), bk[:, t * 128:(t + 1) * 128].rearrange("p (h d) -> p h d", h=2), [bB], [vdstB])

        def attention_pair(p, wv_, wvB, nt, tile0):
            N = nt * 128
            bk, bB = nf()
            for c in range(16):
                MM(bk[:, 0:N], wv_[:, c, 0:128], hT[:, c, 0:N], c == 0, c == 15, [wvB, hTB[c]], [bB])
            A(qT[:, 0:N], bk[:, 0:N], AF.Copy, [bB], [qTB], scale=0.125)
            attn_kv(p, wv_, wvB, nt, kTcur, kTcurB, lambda t: Vcur[:, t, :, 0:64], VcurB)
            if "f" in pend_att:
                pend_att.pop("f")()
            yp, ypB_ = (ypair, ypairB) if p % 2 == 0 else (ktem, ktemB)
            obs = [(pf[4], pfB[4]), (pf[5], pfB[5])]
            for hh in range(2):
                k.dma(SP, biasTr[hh], biasb[hh][:], biasT_d[2 * p + hh], writes=[biasB[hh]])
            steps = []
            for j in range(4 + nt):
                i0 = max(j - 4, 0)
                i1 = min(j, nt - 1)
                if i1 < i0:
                    continue
                for hh in range(2):
                    steps.append((hh, j, i0, i1))

            def src_of(hh, j):
                ps_ = slice(64 * hh, 64 * hh + 64)
                if j < 4:
                    return (kTprev[ps_, p, j * 128:(j + 1) * 128], kTprevB[p], Vprev[:, j, 2 * p + hh, :], VprevB[p])
                return (kTcur[ps_, (j - 4) * 128:(j - 3) * 128], kTcurB, Vcur[:, j - 4, hh, :], VcurB)

            def stageA(st_):
                hh, j, i0, i1 = st_
                ps_ = slice(64 * hh, 64 * hh + 64)
                ncol = (i1 - i0 + 1) * 128
                ksrc, ksB, _, _ = src_of(hh, j)
                sb_, sB = nf()
                MM(sb_[:, 0:ncol], ksrc, qT[ps_, i0 * 128:(i1 + 1) * 128], True, True, [ksB, qTB], [sB])
                pi = cnt["pt"] % NPT
                cnt["pt"] += 1
                b0 = (i0 - j + 4) * 128
                STT(DVE, sb_[:, 0:ncol], sb_[:, 0:ncol], 70.0, biasb[hh][:, b0:b0 + ncol], ALU.min, ALU.add,
                    [sB, biasB[hh]], [sB])
                u = tile0 + j
                A(PT[pi][:, 0:ncol], sb_[:, 0:ncol], AF.Exp, [sB, constB], [PTB[pi]], bias=keymask[:, u:u + 1])
                return pi

            def stageB(st_, pi):
                hh, j, i0, i1 = st_
                _, _, vsrc, vsB = src_of(hh, j)
                ob, oB = obs[hh]
                for i in range(i0, i1 + 1):
                    MM(ob[:, i * 65:(i + 1) * 65], PT[pi][:, (i - i0) * 128:(i - i0 + 1) * 128], vsrc,
                       j == 0 and i == 0, j == 3 + nt and i == nt - 1, [PTB[pi], vsB], [oB])
            pend = []
            for st_ in steps:
                pend.append((st_, stageA(st_)))
                if len(pend) > 3:
                    s0, p0 = pend.pop(0)
                    stageB(s0, p0)
            for s0, p0 in pend:
                stageB(s0, p0)
            for hh in range(2):
                ob, oB = obs[hh]
                ov = ob[:, 0:nt * 65].rearrange("p (t e) -> p t e", e=65)
                TS(DVE, rec[:, hh, 0:nt], ov[:, :, 64], 1e-30, None, ALU.max, None, [oB], [recB[hh]])
                k.op(DVE, lambda hh=hh: nc.vector.reciprocal(out=rec[:, hh, 0:nt], in_=rec[:, hh, 0:nt]), [recB[hh]], [recB[hh]])
                for i in range(nt):
                    if hh == 0:
                        A(yp[:, i, 0:64], ob[:, i * 65:i * 65 + 64], AF.Copy, [oB, recB[hh]], [ypB_],
                          scale=rec[:, hh, i:i + 1])
                    else:
                        TS(DVE, yp[:, i, 64:128], ob[:, i * 65:i * 65 + 64], rec[:, hh, i:i + 1], None,
                           ALU.mult, None, [oB, recB[hh]], [ypB_])
            def finish(p=p, nt=nt, N=N, yp=yp, ypB_=ypB_):
                bk, bB = nb()
                for i in range(nt):
                    TR(bk[:, i * 128:(i + 1) * 128], yp[:, i, :], [ypB_, constB], [bB])
                CP(ACT_, yT[:, p, 0:N], bk[:, 0:N], [bB], [yTB[p]])
            if p < 7:
                pend_att["f"] = finish
            else:
                finish()
            shift_prev(p, nt)

        def shift_prev(p, nt):
            for jj in range(4 - nt):
                CP(POOL, kTprev[:, p, jj * 128:(jj + 1) * 128], kTprev[:, p, (jj + nt) * 128:(jj + nt + 1) * 128],
                   [kTprevB[p]], [kTprevB[p]])
                CP(POOL, Vprev[:, jj, 2 * p:2 * p + 2, :], Vprev[:, jj + nt, 2 * p:2 * p + 2, :], [VprevB[p]], [VprevB[p]])
            CP(POOL, kTprev[:, p, (4 - nt) * 128:512], kTcur[:, 0:nt * 128], [kTcurB], [kTprevB[p]])
            CP(POOL, Vprev[:, 4 - nt:4, 2 * p:2 * p + 2, :], Vcur[:, 0:nt, :, :], [VcurB], [VprevB[p]])

        def gla_head_a(h, wa, waB, nt):
            N = nt * 128
            hs = slice(h * 128, (h + 1) * 128)
            bkb, bbB = nf()
            bke, beB = nf()
            for t in range(nt):
                MM(bkb[:, t * 128:(t + 1) * 128], la[:, t, hs], triu[:, :], True, True, [laB[t], constB], [bbB])
                MM(bke[:, t * 128:(t + 1) * 128], tril[:, :], la[:, t, hs], True, True, [laB[t], constB], [beB])
            A(eb[:, 0:N], bkb[:, 0:N], AF.Exp, [bbB], [ebB])
            A(enb[:, 0:N], bkb[:, 0:N], AF.Exp, [bbB], [enbB], scale=-1.0)
            A(ee2m[:, 0:N], bke[:, 0:N], AF.Exp, [beB], [ee2mB])
            bk, bB = nf()
            for c in range(16):
                MM(bk[:, 0:N], wa[:, c, 0:128], hT[:, c, 0:N], c == 0, c == 15, [waB, hTB[c]], [bB])
            STT(DVE, qdec[:, 0:N], bk[:, 0:N], 128.0 ** -0.5, eb[:, 0:N], ALU.mult, ALU.mult, [bB, ebB], [qdecB])
            bk, bB = nf()
            for c in range(16):
                MM(bk[:, 0:N], wa[:, c, 128:256], hT[:, c, 0:N], c == 0, c == 15, [waB, hTB[c]], [bB])
            TT(DVE, kinv[:, 0:N], bk[:, 0:N], enb[:, 0:N], ALU.mult, [bB, enbB], [kinvB])
            bk, bB = nf()
            for t in range(nt):
                for c in range(16):
                    MM(bk[:, t * 128:(t + 1) * 128], hT[:, c, t * 128:(t + 1) * 128], wa[:, c, 128:256], c == 0, c == 15,
                       [waB, hTB[c]], [bB])
            TT(DVE, ktem[:, 0:nt, :], bk[:, 0:N].rearrange("p (t d) -> p t d", d=128),
               ee2m[:, 0:N].rearrange("p (t d) -> p t d", d=128), ALU.mult, [bB, ee2mB], [ktemB])
            if "f" in pend_tr:
                pend_tr.pop("f")()

        def gla_head_b(h, wb_, wbB, nt):
            N = nt * 128
            for t0 in range(0, nt, 2):
                bk, bB = nf()
                tn = min(2, nt - t0)
                for t in range(t0, t0 + tn):
                    for c in range(16):
                        MM(bk[:, (t - t0) * 256:(t - t0 + 1) * 256], hT[:, c, t * 128:(t + 1) * 128], wb_[:, c, 0:256],
                           c == 0, c == 15, [wbB, hTB[c]], [bB])
                CP(ACT_, vg[:, t0:t0 + tn, :], bk[:, 0:tn * 256].rearrange("p (t d) -> p t d", d=256), [bB], [vgB])
            for t0 in range(0, nt, 2):
                bk, bB = nf()
                tn = min(2, nt - t0)
                for t in range(t0, t0 + tn):
                    for c in range(16):
                        MM(bk[:, (t - t0) * 256:(t - t0 + 1) * 256], hT[:, c, t * 128:(t + 1) * 128], wb_[:, c, 256:512],
                           c == 0, c == 15, [wbB, hTB[c]], [bB])
                A(silr[:, t0:t0 + tn, :], bk[:, 0:tn * 256].rearrange("p (t d) -> p t d", d=256), AF.Silu, [bB], [silrB])
            ba, baB = nf()
            for t in range(nt):
                tsl = slice(t * 128, (t + 1) * 128)
                MM(ba[:, tsl], kinv[:, tsl], qdec[:, tsl], True, True, [kinvB, qdecB], [baB])
            bus = []
            for t0 in range(0, nt, 2):
                bu, buB = nf()
                for t in range(t0, min(t0 + 2, nt)):
                    MM(bu[:, (t - t0) * 256:(t - t0 + 1) * 256], ktem[:, t, :], vg[:, t, :], True, True, [ktemB, vgB], [buB])
                bus.append((bu, buB))
            for t in range(nt):
                TT(DVE, attT4[:, t, :], ba[:, t * 128:(t + 1) * 128], maskT[:], ALU.mult, [baB, constB], [attT4B[t]])
            for t in range(nt):
                CP(ACT_, Sbf4[:, t, :], S[:, h, :], [SB_], [Sbf4B[t]])
                bu, buB = bus[t // 2]
                STT(DVE, S[:, h, :], S[:, h, :], eb[:, t * 128 + 127:t * 128 + 128], bu[:, (t % 2) * 256:(t % 2 + 1) * 256],
                    ALU.mult, ALU.add, [SB_, ebB, buB], [SB_])
            bos = []
            for t0 in range(0, nt, 2):
                bo, boB = nf()
                for t in range(t0, min(t0 + 2, nt)):
                    tsl = slice(t * 128, (t + 1) * 128)
                    osl = slice((t - t0) * 256, (t - t0 + 1) * 256)
                    MM(bo[:, osl], attT4[:, t, :], vg[:, t, :], True, False, [attT4B[t], vgB], [boB])
                    MM(bo[:, osl], qdec[:, tsl], Sbf4[:, t, :], False, True, [qdecB, Sbf4B[t]], [boB])
                bos.append((bo, boB))
            for t in range(nt):
                bo, boB = bos[t // 2]
                osl = slice((t % 2) * 256, (t % 2 + 1) * 256)
                MS(DVE, ss2[:], 0.0, [ss2B])
                A(yg4[:, t, :], bo[:, osl], AF.Square, [boB], [yg4B[t], ss2B], scale=1.0 / 16.0, accum_out=ss2[:])
                RSTD(rstd2, rstd2B, ss2, ss2B)
                STT(DVE, otmp[:], bo[:, osl], rstd2[:, 0:1], gnw[:], ALU.mult, ALU.mult, [boB, rstd2B, constB], [otmpB])
                TT(DVE, yg4[:, t, :], otmp[:], silr[:, t, :], ALU.mult, [otmpB, silrB], [yg4B[t]])

            def finish_tr(h=h, nt=nt, N=N):
                ytb, ytB = nb()
                for t in range(nt):
                    for hf in range(2):
                        TR(ytb[:, hf * 512 + t * 128:hf * 512 + (t + 1) * 128], yg4[:, t, hf * 128:(hf + 1) * 128],
                           [yg4B[t], constB], [ytB])
                for hf in range(2):
                    CP(ACT_, yT[:, 8 + 2 * h + hf, 0:N], ytb[:, hf * 512:hf * 512 + N], [ytB], [yTB[8 + 2 * h + hf]])
            if h < 3:
                pend_tr["f"] = finish_tr
            else:
                finish_tr()

        widx = [i for i, it in enumerate(items) if it[0] is not None]
        loaded = {}
        nxt = 0

        def prefetch(upto):
            nonlocal nxt
            while nxt < len(widx) and nxt <= upto:
                i = widx[nxt]
                loaded[i] = load_w(items[i][0], items[i][2], items[i][3])
                nxt += 1
        wpos = 0
        for i, (parts, fn, key, first) in enumerate(items):
            if max_items is not None and i >= max_items:
                break
            if parts is None:
                fn()
            else:
                prefetch(wpos + NW - 1)
                v, vB, bi, n = loaded.pop(i)
                fn(v, vB)
                wpos += 1

        fin = Buf("fin")
        fin.r = {tr_: tr_.count for tr_ in oT}
        for tr in dbg_outs.values():
            fin.r[tr] = tr.count
        k._deps(SP, [], [fin])
        build.stats = {e.name: (e.nops, e.nwaits) for e in k.engs}
    return nc


def _host_inputs(inputs):
    f = np.float32
    x = np.asarray(inputs["x"], f).reshape(T, D)
    p = np.asarray(inputs["p"], f).reshape(T, 256)
    table = np.asarray(inputs["att_rel_bias"], f).reshape(16, 513)
    kk = np.arange(128)[:, None]
    qq = np.arange(640)[None, :]
    dist = np.clip(qq - kk, -256, 256) + 256
    kc = kk // 64
    qc = qq // 64
    valid = (qc - kc >= 0) & (qc - kc <= 8)
    biasT = np.where(valid[None], table[:, dist], f(NEG)).astype(f)
    ident = np.eye(128, dtype=f)
    ii = np.arange(128)
    maskT = (ii[None, :] >= ii[:, None]).astype(f)
    triu = np.where(ii[:, None] <= ii[None, :], f(-1.0 / 16.0), f(0.0)).astype(f)
    tril = np.where(ii[:, None] > ii[None, :], f(-1.0 / 16.0), f(0.0)).astype(f)

    def col16(v):
        return np.ascontiguousarray(np.asarray(v, f).reshape(16, 128).T)

    convw = np.ascontiguousarray(np.asarray(inputs["w_ffn_conv"], f).reshape(3, 88, 128).transpose(2, 0, 1))
    convb = np.ascontiguousarray(np.asarray(inputs["b_ffn_conv"], f).reshape(88, 128).T)
    gnw = np.ascontiguousarray(np.broadcast_to(np.asarray(inputs["gla_norm_w"], f).reshape(1, 256), (128, 256)))
    fnw = np.ascontiguousarray(np.broadcast_to(np.asarray(inputs["final_norm_w"], f).reshape(1, D), (128, D)))
    wup = np.zeros((32, 512), f)
    wup[0:16] = np.asarray(inputs["w_gla_gate_up"], f).reshape(16, 512)
    wup[16] = np.asarray(inputs["b_gla_gate"], f).reshape(512)
    shared = {
        "biasT": biasT, "ident": ident, "maskT": maskT, "triu": triu, "tril": tril,
        "nmix": col16(inputs["norm_mix_w"]), "nffn": col16(inputs["norm_ffn_w"]), "nple": col16(inputs["norm_ple_w"]),
        "convw": convw, "convb": convb, "gnw": gnw, "fnw": fnw, "wupaug": wup,
        "w_in": np.ascontiguousarray(np.asarray(inputs["w_in"], f).reshape(D, 6160)),
        "w_out": np.ascontiguousarray(np.asarray(inputs["w_out"], f).reshape(D, D)),
        "w_ffn_up": np.ascontiguousarray(np.asarray(inputs["w_ffn_up"], f).reshape(D, 2 * DFF)),
        "w_ffn_down": np.ascontiguousarray(np.asarray(inputs["w_ffn_down"], f).reshape(DFF, D)),
        "w_ple_gate": np.ascontiguousarray(np.asarray(inputs["w_ple_gate"], f).reshape(D, D)),
        "w_ple_proj": np.ascontiguousarray(np.asarray(inputs["w_ple_proj"], f).reshape(256, D)),
    }
    in_maps = []
    for c in range(NCORE):
        m0 = TOK * c - 128
        start = m0 - PREF
        xcc = np.zeros((XC, D), f)
        lo = max(0, -start)
        xcc[lo:] = x[start + lo:start + XC]
        pcc = np.zeros((MAIN, 256), f)
        lo2 = max(0, -m0)
        pcc[lo2:] = p[m0 + lo2:m0 + MAIN]
        km = np.zeros((128, 4 + MAIN_TILES), f)
        tok = (m0 - 512) + 128 * np.arange(4 + MAIN_TILES)[None, :] + np.arange(128)[:, None]
        km[tok < 0] = NEG
        d = dict(shared)
        d["xc"] = xcc
        d["pc"] = pcc
        d["keymask"] = km
        in_maps.append(d)
    return in_maps


_NC = None


def kernel(**inputs):
    global _NC
    in_maps = _host_inputs(inputs)
    if _NC is None:
        _NC = build()
    res = run_bass_kernel_spmd(_NC, in_maps, core_ids=list(range(NCORE)))
    outs = [np.asarray(res.results[c]["out"], np.float32).reshape(TOK, D) for c in range(NCORE)]
    return np.concatenate(outs, axis=0).reshape(1, T, D)
```

```python
import math
import numpy as np
from contextlib import ExitStack
import concourse.bass as bass
import concourse.mybir as mybir
from concourse.bass_utils import run_bass_kernel_spmd

F32 = mybir.dt.float32
BF16 = mybir.dt.bfloat16
AF = mybir.ActivationFunctionType
ALU = mybir.AluOpType

D = 2048
T = 16384
NCORE = 8
TOK = T // NCORE
NPRE_ST = 28
PREF = NPRE_ST * 512
MAIN_TILES = 17
MAIN = MAIN_TILES * 128
XC = PREF + MAIN
EPS = 1e-6
NEG = -30000.0
DFF = 5632
NFC = DFF // 128


class Buf:
    __slots__ = ("name", "w", "r", "psum")

    def __init__(self, name="", psum=False):
        self.name = name
        self.w = None
        self.r = {}
        self.psum = psum


class Track:
    def __init__(self, sem, name):
        self.sem = sem
        self.count = 0
        self.name = name


class Eng:
    def __init__(self, k, name, eng):
        self.name = name
        self.eng = eng
        self.track = Track(k.new_sem("e_" + name), name)
        self.waited = {}
        self.nwaits = 0
        self.nops = 0


class K:
    def __init__(self, nc, stack):
        self.nc = nc
        self.stack = stack
        self.pe = Eng(self, "pe", nc.tensor)
        self.act = Eng(self, "act", nc.scalar)
        self.dve = Eng(self, "dve", nc.vector)
        self.pool = Eng(self, "pool", nc.gpsimd)
        self.sp = Eng(self, "sp", nc.sync)
        self.engs = [self.pe, self.act, self.dve, self.pool, self.sp]

    def new_sem(self, name):
        return self.stack.enter_context(self.nc.semaphore(name))

    def dma_track(self, name):
        return Track(self.new_sem("d_" + name), name)

    def sb(self, name, shape, dtype):
        return self.stack.enter_context(self.nc.sbuf_tensor("s_" + name, list(shape), dtype))

    def ps(self, name, shape, dtype=F32):
        return self.stack.enter_context(self.nc.psum_tensor("p_" + name, list(shape), dtype))

    def _deps(self, e, reads, writes, skip_own=False):
        need = {}

        def add(tc):
            if tc is None:
                return
            t, c = tc
            if need.get(t, 0) < c:
                need[t] = c

        for b in reads:
            add(b.w)
            if b.psum:
                for t, c in b.r.items():
                    if t is not e.track:
                        add((t, c))
        for b in writes:
            add(b.w)
            for t, c in b.r.items():
                add((t, c))
        for t, c in need.items():
            if skip_own and t is e.track:
                continue
            if e.waited.get(t, 0) >= c:
                continue
            e.eng.wait_ge(t.sem, c)
            e.waited[t] = c
            e.nwaits += 1

    def op(self, e, fn, reads=(), writes=(), pe_acc=False):
        self._deps(e, reads, writes, skip_own=(pe_acc or e is self.pe))
        ins = fn()
        t = e.track
        t.count += 1
        ins.then_inc(t.sem, 1)
        e.nops += 1
        for b in reads:
            b.r[t] = t.count
        for b in writes:
            b.w = (t, t.count)
            b.r = {}
        return ins

    def dma(self, e, track, out, in_, reads=(), writes=()):
        self._deps(e, reads, writes)
        ins = e.eng.dma_start(out=out, in_=in_)
        track.count += 16
        ins.then_inc(track.sem, 16)
        for b in reads:
            b.r[track] = track.count
        for b in writes:
            b.w = (track, track.count)
            b.r = {}
        return ins


class _Stop(Exception):
    pass


def build(dbg=None, npre_run=None, max_items=None, stop=None):
    nc = bass.Bass("TRN2", target_bir_lowering=False)

    def din(name, shape):
        return nc.dram_tensor(name, list(shape), F32, kind="ExternalInput").ap()

    xc = din("xc", [XC, D])
    pc = din("pc", [MAIN, 256])
    keymask_d = din("keymask", [128, 4 + MAIN_TILES])
    biasT_d = din("biasT", [16, 128, 640])
    ident_d = din("ident", [128, 128])
    maskT_d = din("maskT", [128, 128])
    triu_d = din("triu", [128, 128])
    tril_d = din("tril", [128, 128])
    nmix_d = din("nmix", [128, 16])
    nffn_d = din("nffn", [128, 16])
    nple_d = din("nple", [128, 16])
    convw_d = din("convw", [128, 3, 88])
    convb_d = din("convb", [128, 88])
    gnw_d = din("gnw", [128, 256])
    fnw_d = din("fnw", [128, D])
    wup_d = din("wupaug", [32, 512])
    w_in = din("w_in", [D, 6160]).rearrange("(c p) n -> p c n", p=128)
    w_out = din("w_out", [D, D]).rearrange("(c p) n -> p c n", p=128)
    w_up = din("w_ffn_up", [D, 2 * DFF]).rearrange("(c p) n -> p c n", p=128)
    w_down = din("w_ffn_down", [DFF, D]).rearrange("(c p) n -> p c n", p=128)
    w_pg = din("w_ple_gate", [D, D]).rearrange("(c p) n -> p c n", p=128)
    w_pp = din("w_ple_proj", [256, D]).rearrange("(c p) n -> p c n", p=128)
    out_d = nc.dram_tensor("out", [TOK, D], F32, kind="ExternalOutput").ap()
    dbg_outs = {}

    with ExitStack() as st:
        k = K(nc, st)
        PE, ACT_, DVE, POOL, SP = k.pe, k.act, k.dve, k.pool, k.sp

        def ckg(n):
            if stop == n:
                raise _Stop()

        def A(out, in_, func, r, w, **kw):
            return k.op(ACT_, lambda: nc.scalar.activation(out=out, in_=in_, func=func, **kw), r, w)

        def veng(e):
            return nc.vector if e is DVE else nc.gpsimd

        def TS(e, out, in0, s1, s2, op0, op1, r, w):
            if s2 is None:
                return k.op(e, lambda: veng(e).tensor_scalar(out=out, in0=in0, scalar1=s1, scalar2=None, op0=op0), r, w)
            return k.op(e, lambda: veng(e).tensor_scalar(out=out, in0=in0, scalar1=s1, scalar2=s2, op0=op0, op1=op1), r, w)

        def TT(e, out, in0, in1, op, r, w):
            return k.op(e, lambda: veng(e).tensor_tensor(out=out, in0=in0, in1=in1, op=op), r, w)

        def STT(e, out, in0, scalar, in1, op0, op1, r, w):
            return k.op(e, lambda: veng(e).scalar_tensor_tensor(out=out, in0=in0, scalar=scalar, in1=in1, op0=op0, op1=op1), r, w)

        def CP(e, out, in_, r, w):
            if e is ACT_:
                return k.op(e, lambda: nc.scalar.copy(out=out, in_=in_), r, w)
            return k.op(e, lambda: veng(e).tensor_copy(out=out, in_=in_), r, w)

        def MS(e, ap, val, w):
            return k.op(e, lambda: veng(e).memset(ap, val), (), w)

        def MM(out, lhsT, rhs, start, stop, r, w):
            return k.op(PE, lambda: nc.tensor.matmul(out, lhsT=lhsT, rhs=rhs, start=start, stop=stop), r, w,
                        pe_acc=not start)

        def RSTD(dst, dstB, src, srcB):
            ckg(10)
            TS(DVE, dst[:], src[:], EPS, None, ALU.add, None, [srcB], [dstB])
            ckg(11)
            A(dst[:], dst[:], AF.Ln, [dstB], [dstB])
            ckg(12)
            A(dst[:], dst[:], AF.Exp, [dstB], [dstB], scale=-0.5)
            ckg(13)

        def TR(out, in_, r, w):
            return k.op(PE, lambda: nc.tensor.transpose(out, in_, ident[:]), list(r) + [constSB], w)

        pf = [k.ps(f"pf{i}", [128, 512], F32) for i in range(6)]
        pfB = [Buf(f"pf{i}", psum=True) for i in range(6)]
        pb = [k.ps(f"pb{i}", [128, 1024], BF16) for i in range(2)]
        pbB = [Buf(f"pb{i}", psum=True) for i in range(2)]
        cnt = {"f": 0, "b": 0, "w": 0, "pt": 0}

        def nf():
            i = cnt["f"] % 4
            cnt["f"] += 1
            return pf[i], pfB[i]

        def nb():
            i = cnt["b"] % 2
            cnt["b"] += 1
            return pb[i], pbB[i]

        constB = Buf("const")
        tconst = k.dma_track("const")
        ident = k.sb("ident", [128, 128], BF16)
        maskT = k.sb("maskT", [128, 128], F32)
        triu = k.sb("triu", [128, 128], F32)
        tril = k.sb("tril", [128, 128], F32)
        nmix = k.sb("nmix", [128, 16], F32)
        nffn = k.sb("nffn", [128, 16], F32)
        nple = k.sb("nple", [128, 16], F32)
        convw = k.sb("convw", [128, 3, 88], F32)
        convb = k.sb("convb", [128, 88], F32)
        gnw = k.sb("gnw", [128, 256], F32)
        keymask = k.sb("keymask", [128, 4 + MAIN_TILES], F32)
        wupaug = k.sb("wupaug", [32, 512], BF16)
        constSB = Buf("constS")
        tconsts = k.dma_track("consts")
        k.dma(POOL, tconsts, ident[:], ident_d, writes=[constSB])
        k.dma(POOL, tconsts, wupaug[:], wup_d, writes=[constSB])
        for dst, src in [(maskT, maskT_d), (triu, triu_d), (tril, tril_d), (nmix, nmix_d), (nffn, nffn_d),
                         (nple, nple_d), (convw, convw_d), (convb, convb_d), (gnw, gnw_d), (keymask, keymask_d)]:
            k.dma(SP, tconst, dst[:], src, writes=[constB])

        S = k.sb("S", [128, 4, 256], F32)
        SB_ = Buf("S")
        MS(DVE, S[:], 0.0, [SB_])
        hT = k.sb("hT", [128, 16, 512], BF16)
        hTB = [Buf(f"hT{c}") for c in range(16)]
        htmp = k.sb("htmp", [128, D], BF16)
        htmpB = Buf("htmp")
        ss = k.sb("ss", [128, 1], F32)
        ssB = Buf("ss")
        rstd = k.sb("rstd", [128, 1], F32)
        rstdB = Buf("rstd")
        glT = k.sb("glT", [32, 512], BF16)
        glTB = Buf("glT")
        MS(DVE, glT[:], 1.0, [glTB])
        la = k.sb("la", [128, 4, 512], F32)
        laB = [Buf(f"la{t}") for t in range(4)]
        gtmp = k.sb("gtmp", [128, 512], F32)
        gtmpB = Buf("gtmp")

        def dbg_dump(name, ap, shape, rbufs, dtype=F32):
            if dbg is None or name not in dbg:
                return
            d = nc.dram_tensor("dbg_" + name, list(shape), dtype, kind="ExternalOutput").ap()
            tr = k.dma_track("dbg_" + name)
            k.dma(SP, tr, d, ap, reads=rbufs)
            dbg_outs[name] = tr

        class NS:
            pass
        M = NS()
        M.hT, M.hTB, M.htmp, M.htmpB, M.ss, M.ssB, M.rstd, M.rstdB = hT, hTB, htmp, htmpB, ss, ssB, rstd, rstdB
        M.la, M.laB, M.glT, M.glTB, M.gtmp, M.gtmpB = la, laB, glT, glTB, gtmp, gtmpB
        M.htmpL, M.htmpLB = [htmp], [htmpB]

        def norm_partA(xa, xB, ns, hi=0):
            htmp_, htmpB_ = ns.htmpL[hi], ns.htmpLB[hi]
            MS(DVE, ns.ss[:], 0.0, [ns.ssB])
            A(htmp_[:], xa, AF.Square, [xB], [htmpB_, ns.ssB], scale=1.0 / math.sqrt(D), accum_out=ns.ss[:])
            RSTD(ns.rstd, ns.rstdB, ns.ss, ns.ssB)
            TS(DVE, htmp_[:], xa, ns.rstd[:, 0:1], None, ALU.mult, None, [xB, ns.rstdB], [htmpB_])

        def norm_partB(t, wcol, ns, hi=0):
            htmp_, htmpB_ = ns.htmpL[hi], ns.htmpLB[hi]
            for half in range(2):
                bk, bB = nb()
                for j in range(8):
                    c = half * 8 + j
                    TR(bk[:, j * 128:(j + 1) * 128], htmp_[:, c * 128:(c + 1) * 128], [htmpB_, constB], [bB])
                for j in range(8):
                    c = half * 8 + j
                    dst = ns.hT[:, c, t * 128:(t + 1) * 128]
                    src = bk[:, j * 128:(j + 1) * 128]
                    if half == 0:
                        A(dst, src, AF.Copy, [bB, constB], [ns.hTB[c]], scale=wcol[:, c:c + 1])
                    else:
                        TS(DVE, dst, src, wcol[:, c:c + 1], None, ALU.mult, None, [bB, constB], [ns.hTB[c]])

        def norm_tile(xa, xB, t, wcol, ns):
            norm_partA(xa, xB, ns)
            norm_partB(t, wcol, ns)

        def norm_hT(xs, wcol, nt, ns=None):
            ns = ns or M
            junk = gact[:, 0:4, :].rearrange("p a b -> p (a b)")
            junkB = gactB[0:4]

            def a1(t):
                xa, xB = xs[t]
                MS(DVE, ns.ss[:], 0.0, [ns.ssB])
                A(junk, xa, AF.Square, [xB], junkB + [ns.ssB], scale=1.0 / math.sqrt(D), accum_out=ns.ss[:])
                RSTD(ns.rstd, ns.rstdB, ns.ss, ns.ssB)

            def a2(t):
                xa, xB = xs[t]
                TS(DVE, ns.htmp[:], xa, ns.rstd[:, 0:1], None, ALU.mult, None, [xB, ns.rstdB], [ns.htmpB])
            a1(0)
            a2(0)
            for t in range(nt):
                if t + 1 < nt:
                    a1(t + 1)
                norm_partB(t, wcol, ns)
                if t + 1 < nt:
                    a2(t + 1)

        def gate_g1(wg, wgB, nt, ns):
            N = nt * 128
            bk, bB = nf()
            for c in range(16):
                MM(bk[0:16, 0:N], wg[:, c, 0:16], ns.hT[:, c, 0:N], c == 0, c == 15, [wgB, ns.hTB[c]], [bB])
            CP(ACT_, ns.glT[0:16, 0:N], bk[0:16, 0:N], [bB], [ns.glTB])

        def gate_g2(nt, ns):
            for t in range(nt):
                bk, bB = nf()
                MM(bk[:, :], ns.glT[0:32, t * 128:(t + 1) * 128], wupaug[0:32, :], True, True, [ns.glTB, constSB], [bB])
                A(ns.gtmp[:], bk[:, :], AF.Exp, [bB], [ns.gtmpB], scale=-1.0)
                A(ns.la[:, t, :], ns.gtmp[:], AF.Ln, [ns.gtmpB], [ns.laB[t]], bias=1.0)

        def gate_chain(wg, wgB, nt, ns=None):
            ns = ns or M
            gate_g1(wg, wgB, nt, ns)
            gate_g2(nt, ns)

        HALF = NFC // 2
        scr = {}
        items = []

        seqc = {"n": 0, "first": True, "use": False}

        def add(parts, fn, skip=False):
            if skip:
                if parts is not None and seqc["use"]:
                    seqc["n"] += 1
                return
            if parts is None or not seqc["use"]:
                items.append((parts, fn, None, True))
            else:
                items.append((parts, fn, seqc["n"], seqc["first"]))
                seqc["n"] += 1

        def main_st(tile0, nt, store):
            N = nt * 128
            r0 = PREF + tile0 * 128
            seqc["n"] = 0
            seqc["first"] = (tile0 == 1)
            seqc["use"] = True

            def stage_load():
                for t in range(nt):
                    k.dma(SP, xT_[t], xres[t][:], xc[r0 + t * 128:r0 + (t + 1) * 128, :], writes=[xresB[t]])
                norm_hT([(xres[t][:], xresB[t]) for t in range(nt)], nmix, nt)
            add(None, stage_load)
            for p in range(8):
                add([w_in[:, :, 128 * p:128 * p + 128], w_in[:, :, 1024 + 128 * p:1024 + 128 * p + 128],
                     w_in[:, :, 2048 + 128 * p:2048 + 128 * p + 128]],
                    lambda v, vB, p=p: attention_pair(p, wjoin(v), vB, nt, tile0))
            add([w_in[:, :, 6144:6160]], lambda v, vB: gate_chain(v[0], vB, nt))
            for h in range(4):
                add([w_in[:, :, 3072 + 128 * h:3072 + 128 * h + 128], w_in[:, :, 3584 + 128 * h:3584 + 128 * h + 128]],
                    lambda v, vB, h=h: gla_head_a(h, wjoin(v), vB, nt))
                add([w_in[:, :, 4096 + 256 * h:4096 + 256 * h + 256], w_in[:, :, 5120 + 256 * h:5120 + 256 * h + 256]],
                    lambda v, vB, h=h: gla_head_b(h, wjoin(v), vB, nt))
            for g in range(4):
                def f_out(v, vB, g=g):
                    for t in range(nt):
                        bk, bB = nf()
                        for c in range(16):
                            MM(bk[:, :], yT[:, c, t * 128:(t + 1) * 128], v[0][:, c, :], c == 0, c == 15, [yTB[c], vB], [bB])
                        TT(DVE, xres[t][:, g * 512:(g + 1) * 512], xres[t][:, g * 512:(g + 1) * 512], bk[:, :], ALU.add,
                           [xresB[t], bB], [xresB[t]])
                add([w_out[:, :, g * 512:(g + 1) * 512]], f_out)
            add(None, lambda: norm_hT([(xres[t][:], xresB[t]) for t in range(nt)], nffn, nt))
            for half in range(2):
                for blk in range(HALF // 2):
                    def f_up(v, vB, half=half, blk=blk):
                        for jj in range(2):
                            jq = blk * 2 + jj
                            j = half * HALF + jq
                            for br in range(2):
                                ch = j + br * NFC
                                bk, bB = nf()
                                for c in range(16):
                                    MM(bk[:, 0:N], v[br][:, c, jj * 128:(jj + 1) * 128], hT[:, c, 0:N], c == 0, c == 15,
                                       [vB, hTB[c]], [bB])
                                CP(POOL, ext[br][:, 0:2], tail[:, ch, :], [tailB], [extB[br]])
                                CP(ACT_, ext[br][:, 2:2 + N], bk[:, 0:N], [bB], [extB[br]])
                                CP(POOL, tail[:, ch, :], ext[br][:, N:N + 2], [extB[br]], [tailB])
                                TS(DVE, ycv[br][:, 0:N], ext[br][:, 2:2 + N], convw[:, 2, ch:ch + 1], convb[:, ch:ch + 1],
                                   ALU.mult, ALU.add, [extB[br], constB], [ycvB[br]])
                                STT(DVE, ycv[br][:, 0:N], ext[br][:, 1:1 + N], convw[:, 1, ch:ch + 1], ycv[br][:, 0:N],
                                    ALU.mult, ALU.add, [extB[br], constB, ycvB[br]], [ycvB[br]])
                                STT(DVE, ycv[br][:, 0:N], ext[br][:, 0:N], convw[:, 0, ch:ch + 1], ycv[br][:, 0:N],
                                    ALU.mult, ALU.add, [extB[br], constB, ycvB[br]], [ycvB[br]])
                            A(ycv[0][:, 0:N], ycv[0][:, 0:N], AF.Gelu, [ycvB[0]], [ycvB[0]])
                            TT(DVE, gact[:, jq, 0:N], ycv[0][:, 0:N], ycv[1][:, 0:N], ALU.mult, [ycvB[0], ycvB[1]],
                               [gactB[jq]])
                    c0 = (half * HALF + blk * 2) * 128
                    add([w_up[:, :, c0:c0 + 256], w_up[:, :, DFF + c0:DFF + c0 + 256]], f_up)
                banks = {}
                for g in range(4):
                    for kg in range(2):
                        def f_dn(v, vB, g=g, kg=kg, half=half):
                            for t in range(nt):
                                if kg == 0:
                                    banks[(g, t)] = nf()
                                bk, bB = banks[(g, t)]
                                for jj in range(11):
                                    jq = kg * 11 + jj
                                    MM(bk[:, :], gact[:, jq, t * 128:(t + 1) * 128], v[0][:, jj, :],
                                       kg == 0 and jj == 0, kg == 1 and jj == 10, [gactB[jq], vB], [bB])
                                if kg == 1:
                                    TT(DVE, xres[t][:, g * 512:(g + 1) * 512], xres[t][:, g * 512:(g + 1) * 512], bk[:, :],
                                       ALU.add, [xresB[t], bB], [xresB[t]])
                        f0 = half * HALF + kg * 11
                        add([w_down[:, f0:f0 + 11, g * 512:(g + 1) * 512]], f_dn, skip=not store)
            def stage_ple():
                norm_hT([(xres[t][:], xresB[t]) for t in range(nt)], nple, nt)
                bk, bB = nb()
                for t in range(nt):
                    pr0 = tile0 * 128 + t * 128
                    k.dma(SP, pinT, pin[:], pc[pr0:pr0 + 128, :], writes=[pinB])
                    CP(DVE, pbf[:], pin[:], [pinB], [pbfB])
                    for kc in range(2):
                        TR(bk[:, kc * 512 + t * 128:kc * 512 + (t + 1) * 128], pbf[:, kc * 128:(kc + 1) * 128],
                           [pbfB, constB], [bB])
                for kc in range(2):
                    CP(ACT_, pT[:, kc, 0:N], bk[:, kc * 512:kc * 512 + N], [bB], [pTB])
            add(None, stage_ple, skip=not store)
            for g in range(8):
                def f_ple(v, vB, g=g):
                    for t in range(nt):
                        bk, bB = nf()
                        for c in range(16):
                            MM(bk[:, 0:256], hT[:, c, t * 128:(t + 1) * 128], v[0][:, c, :], c == 0, c == 15, [hTB[c], vB], [bB])
                        A(gsig[:, 0:256], bk[:, 0:256], AF.Sigmoid, [bB], [gsigB])
                        b2, b2B = nf()
                        for kc in range(2):
                            MM(b2[:, 0:256], pT[:, kc, t * 128:(t + 1) * 128], v[1][:, kc, :], kc == 0, kc == 1, [pTB, vB], [b2B])
                        TT(DVE, gsig[:, 0:256], gsig[:, 0:256], b2[:, 0:256], ALU.mult, [gsigB, b2B], [gsigB])
                        TT(DVE, xres[t][:, g * 256:(g + 1) * 256], xres[t][:, g * 256:(g + 1) * 256], gsig[:, 0:256], ALU.add,
                           [xresB[t], gsigB], [xresB[t]])
                add([w_pg[:, :, g * 256:(g + 1) * 256], w_pp[:, :, g * 256:(g + 1) * 256]], f_ple, skip=not store)
            def stage_final():
                if not store:
                    return
                for t in range(nt):
                    MS(DVE, ss[:], 0.0, [ssB])
                    A(htmp[:], xres[t][:], AF.Square, [xresB[t]], [htmpB, ssB], scale=1.0 / math.sqrt(D), accum_out=ss[:])
                    RSTD(rstd, rstdB, ss, ssB)
                    STT(DVE, xres[t][:], xres[t][:], rstd[:, 0:1], fnw[:], ALU.mult, ALU.mult, [xresB[t], rstdB, fnwB],
                        [xresB[t]])
                    o0 = (tile0 - 1 + t) * 128
                    k.dma(SP, oT[t], out_d[o0:o0 + 128, :], xres[t][:], reads=[xresB[t]])
            add(None, stage_final, skip=not store)

        def wjoin(v):
            return WJ(v)

        class WJ:
            def __init__(self, parts):
                self.parts = parts
                self.offs = []
                o = 0
                for pp in parts:
                    self.offs.append(o)
                    o += pp.shape[2]
                self.n = o

            def __getitem__(self, key):
                ps_, c, cols = key
                a, b = cols.start, cols.stop
                for pp, o in zip(self.parts, self.offs):
                    if a >= o and b <= o + pp.shape[2]:
                        return pp[ps_, c, a - o:b - o]
                raise IndexError((a, b))

        def stage_halo():
            for t in range(4):
                r0 = PREF - 512 + t * 128
                k.dma(SP, xT_[t], xres[t][:], xc[r0:r0 + 128, :], writes=[xresB[t]])
            norm_hT([(xres[t][:], xresB[t]) for t in range(4)], nmix, 4)
        add(None, stage_halo)
        for p in range(8):
            def f_halo(v, vB, p=p):
                wv_ = WJ([v[0], v[0], v[1]])
                attn_kv(p, wv_, vB, 4, kTprev[:, p, :], kTprevB[p], lambda t: Vprev[:, t, 2 * p:2 * p + 2, 0:64], VprevB[p])
            add([w_in[:, :, 1024 + 128 * p:1024 + 128 * p + 128], w_in[:, :, 2048 + 128 * p:2048 + 128 * p + 128]], f_halo)
        main_st(0, 1, False)
        for s_ in range(4):
            main_st(1 + 4 * s_, 4, True)

        with ExitStack() as pst:
            def psb(name, shape, dtype):
                return pst.enter_context(nc.sbuf_tensor("s_" + name, list(shape), dtype))

            wpre = psb("wpre", [128, 16, 1552], BF16)
            wpreB = Buf("wpre")
            twp = k.dma_track("wpre")
            k.dma(POOL, twp, wpre[:, :, 0:512], w_in[:, :, 3584:4096], writes=[wpreB])
            k.dma(POOL, twp, wpre[:, :, 512:1024], w_in[:, :, 4096:4608], writes=[wpreB])
            k.dma(POOL, twp, wpre[:, :, 1024:1536], w_in[:, :, 4608:5120], writes=[wpreB])
            k.dma(POOL, twp, wpre[:, :, 1536:1552], w_in[:, :, 6144:6160], writes=[wpreB])
            NBG = 8
            tbg = [k.dma_track(f"bg{i}") for i in range(NBG)]
            for (parts_, fn_, key_, first_) in items:
                if key_ is None or not first_:
                    continue
                n_ = sum(pp.shape[1] * pp.shape[2] for pp in parts_)
                d_ = nc.dram_tensor(f"scr{key_}", [128, n_], BF16).ap()
                off_ = 0
                for src_ in parts_:
                    nch_, ncol_ = src_.shape[1], src_.shape[2]
                    k.dma(POOL, tbg[key_ % NBG], d_[:, off_:off_ + nch_ * ncol_].rearrange("p (c n) -> p c n", c=nch_), src_)
                    off_ += nch_ * ncol_
                scr[key_] = d_
            NXP = 4
            xp = [psb(f"xp{i}", [128, D], F32) for i in range(NXP)]
            xpB = [Buf(f"xp{i}") for i in range(NXP)]
            xpT = [k.dma_track(f"xp{i}") for i in range(NXP)]
            PS = []
            for par in range(2):
                ns = NS()
                if par == 0:
                    ns.__dict__.update(M.__dict__)
                    ns.htmpL = [htmp, psb("htmp_a1", [128, D], BF16)]
                    ns.htmpLB = [htmpB, Buf("htmp_a1")]
                else:
                    ns.hT = psb("hT_b", [128, 16, 512], BF16)
                    ns.hTB = [Buf(f"hTb{c}") for c in range(16)]
                    ns.htmpL = [psb(f"htmp_b{i}", [128, D], BF16) for i in range(2)]
                    ns.htmpLB = [Buf(f"htmp_b{i}") for i in range(2)]
                    ns.ss = psb("ss_b", [128, 1], F32)
                    ns.ssB = Buf("ss_b")
                    ns.rstd = psb("rstd_b", [128, 1], F32)
                    ns.rstdB = Buf("rstd_b")
                    ns.la = psb("la_b", [128, 4, 512], F32)
                    ns.laB = [Buf(f"la_b{t}") for t in range(4)]
                    ns.glT = psb("glT_b", [32, 512], BF16)
                    ns.glTB = Buf("glT_b")
                    MS(DVE, ns.glT[:], 1.0, [ns.glTB])
                    ns.gtmp = psb("gtmp_b", [128, 512], F32)
                    ns.gtmpB = Buf("gtmp_b")
                ns.ee2 = [psb(f"ee2_{par}{i}", [128, 512], F32) for i in range(2)]
                ns.ee2B = [Buf(f"ee2_{par}{i}") for i in range(2)]
                ns.decp = [psb(f"decp_{par}{i}", [128, 4], F32) for i in range(2)]
                ns.decpB = [Buf(f"decp_{par}{i}") for i in range(2)]
                ns.kte = [psb(f"kte_{par}{i}", [128, 512], BF16) for i in range(2)]
                ns.kteB = [Buf(f"kte_{par}{i}") for i in range(2)]
                ns.vpre = [psb(f"vpre_{par}{i}", [128, 1024], BF16) for i in range(2)]
                ns.vpreB = [Buf(f"vpre_{par}{i}") for i in range(2)]
                PS.append(ns)
            xcnt = 0
            wgp = wpre[:, :, 1536:1552]

            def ck(n):
                if stop == n:
                    raise _Stop()

            def pA(sti, t, ns, hi):
                nonlocal xcnt
                i = xcnt % NXP
                xcnt += 1
                r0 = sti * 512 + t * 128
                k.dma(SP, xpT[i], xp[i][:], xc[r0:r0 + 128, :], writes=[xpB[i]])
                norm_partA(xp[i][:], xpB[i], ns, hi)

            def mm_tile(t, ns):
                tsl = slice(t * 128, (t + 1) * 128)
                q = t % 2
                bk, bB = nf()
                MM(bk[:, :], tril[:, :], ns.la[:, t, :], True, True, [constB, ns.laB[t]], [bB])
                A(ns.ee2[q][:], bk[:, :], AF.Exp, [bB], [ns.ee2B[q]])
                bk, bB = nf()
                for h in range(4):
                    MM(bk[:, h:h + 1], ns.la[:, t, h * 128:(h + 1) * 128], triu[:, 127:128], True, True,
                       [constB, ns.laB[t]], [bB])
                A(ns.decp[q][:], bk[:, 0:4], AF.Exp, [bB], [ns.decpB[q]])
                bk, bB = nf()
                for c in range(16):
                    MM(bk[:, :], ns.hT[:, c, tsl], wpre[:, c, 0:512], c == 0, c == 15, [ns.hTB[c], wpreB], [bB])
                TT(DVE, ns.kte[q][:], bk[:, :], ns.ee2[q][:], ALU.mult, [bB, ns.ee2B[q]], [ns.kteB[q]])
                for vh in range(2):
                    bk, bB = nf()
                    for c in range(16):
                        MM(bk[:, :], ns.hT[:, c, tsl], wpre[:, c, 512 + vh * 512:1024 + vh * 512], c == 0, c == 15,
                           [ns.hTB[c], wpreB], [bB])
                    CP(ACT_, ns.vpre[q][:, vh * 512:(vh + 1) * 512], bk[:, :], [bB], [ns.vpreB[q]])
                for h in range(4):
                    bk, bB = nf()
                    MM(bk[:, 0:256], ns.kte[q][:, h * 128:(h + 1) * 128], ns.vpre[q][:, h * 256:(h + 1) * 256], True, True,
                       [ns.kteB[q], ns.vpreB[q]], [bB])
                    STT(DVE, S[:, h, :], S[:, h, :], ns.decp[q][:, h:h + 1], bk[:, 0:256], ALU.mult, ALU.add,
                        [SB_, ns.decpB[q], bB], [SB_])
            try:
                sts = list(range(NPRE_ST - (NPRE_ST if npre_run is None else npre_run), NPRE_ST))
                if sts:
                    ns0 = PS[0]
                    for t in range(4):
                        pA(sts[0], t, ns0, t % 2)
                        norm_partB(t, nmix, ns0, t % 2)
                    gate_chain(wgp, wpreB, 4, ns0)
                sched = {0: [(0, 0), (1, 1)], 1: [(2, 0)], 2: [(3, 1)], 3: []}
                for si, sti in enumerate(sts):
                    ns = PS[si % 2]
                    nn = PS[(si + 1) % 2]
                    nxt_ = sts[si + 1] if si + 1 < len(sts) else None
                    for t in range(4):
                        if nxt_ is not None:
                            for (tt, hi) in sched[t]:
                                pA(nxt_, tt, nn, hi)
                            if t == 3:
                                gate_g1(wgp, wpreB, 4, nn)
                        mm_tile(t, ns)
                        if nxt_ is not None:
                            for (tt, hi) in sched[t]:
                                norm_partB(tt, nmix, nn, hi)
                            if t == 3:
                                gate_g2(4, nn)
            except _Stop:
                pass
            for e in k.engs:
                for o in k.engs:
                    if o.track.count > e.waited.get(o.track, 0):
                        e.eng.wait_ge(o.track.sem, o.track.count)
                        e.waited[o.track] = o.track.count
                for tr in xpT + [twp] + tbg:
                    if tr.count > e.waited.get(tr, 0):
                        e.eng.wait_ge(tr.sem, tr.count)
                        e.waited[tr] = tr.count
        dbg_dump("S_in", S[:], [128, 4, 256], [SB_])

        NW = 2
        WSZ = 8192
        wbuf = [k.sb(f"wbuf{i}", [128, WSZ], BF16) for i in range(NW)]
        wB = [Buf(f"wbuf{i}") for i in range(NW)]
        wT = [k.dma_track(f"w{i}") for i in range(NW)]

        wT2 = [k.dma_track(f"wh{i}") for i in range(NW)]

        def load_w(parts, key=None, first=True):
            i = cnt["w"] % NW
            cnt["w"] += 1
            off = 0
            views = []
            for src in parts:
                nch, ncol = src.shape[1], src.shape[2]
                views.append(wbuf[i][:, off:off + nch * ncol].rearrange("p (c n) -> p c n", c=nch))
                off += nch * ncol
            assert off <= WSZ
            if key is None:
                for src, dst in zip(parts, views):
                    k.dma(POOL, wT[i], dst, src, writes=[wB[i]])
            else:
                k.dma(SP, wT2[i], wbuf[i][:, 0:off], scr[key], writes=[wB[i]])
            return views, wB[i], i, off

        xres = [k.sb(f"xres{t}", [128, D], F32) for t in range(4)]
        xresB = [Buf(f"xres{t}") for t in range(4)]
        xT_ = [k.dma_track(f"xres{t}") for t in range(4)]
        yT = k.sb("yT", [128, 16, 512], BF16)
        yTB = [Buf(f"yT{c}") for c in range(16)]
        kTprev = k.sb("kTprev", [128, 8, 512], BF16)
        kTprevB = [Buf(f"kTprev{p}") for p in range(8)]
        Vprev = k.sb("Vprev", [128, 4, 16, 65], BF16)
        VprevB = [Buf(f"Vprev{p}") for p in range(8)]
        MS(DVE, Vprev[:], 1.0, VprevB)
        qT = k.sb("qT", [128, 512], BF16)
        qTB = Buf("qT")
        kTcur = k.sb("kTcur", [128, 512], BF16)
        kTcurB = Buf("kTcur")
        Vcur = k.sb("Vcur", [128, 4, 2, 65], BF16)
        VcurB = Buf("Vcur")
        MS(DVE, Vcur[:], 1.0, [VcurB])
        biasb = [k.sb(f"biasb{i}", [128, 640], F32) for i in range(2)]
        biasB = [Buf(f"biasb{i}") for i in range(2)]
        biasTr = [k.dma_track(f"bias{i}") for i in range(2)]
        NPT = 4
        PT = [k.sb(f"PT{i}", [128, 512], BF16) for i in range(NPT)]
        PTB = [Buf(f"PT{i}") for i in range(NPT)]
        rec = k.sb("rec", [128, 2, 4], F32)
        recB = [Buf("rec0"), Buf("rec1")]
        eb = k.sb("eb", [128, 512], F32)
        ebB = Buf("eb")
        enb = k.sb("enb", [128, 512], F32)
        enbB = Buf("enb")
        ee2m = k.sb("ee2m", [128, 512], F32)
        ee2mB = Buf("ee2m")
        qdec = k.sb("qdec", [128, 512], BF16)
        qdecB = Buf("qdec")
        ypair = qdec[:].rearrange("p (t d) -> p t d", d=128)
        ypairB = qdecB
        kinv = kTcur
        kinvB = kTcurB
        ktem = k.sb("ktem", [128, 4, 128], BF16)
        ktemB = Buf("ktem")
        vg = k.sb("vg", [128, 4, 256], BF16)
        vgB = Buf("vg")
        silr = k.sb("silr", [128, 4, 256], F32)
        silrB = Buf("silr")
        Sbf4 = k.sb("Sbf4", [128, 4, 256], BF16)
        Sbf4B = [Buf(f"Sbf4_{t}") for t in range(4)]
        attT4 = k.sb("attT4", [128, 4, 128], BF16)
        attT4B = [Buf(f"attT4_{t}") for t in range(4)]
        otmp = k.sb("otmp", [128, 256], F32)
        otmpB = Buf("otmp")
        yg4 = k.sb("yg4", [128, 4, 256], BF16)
        yg4B = [Buf(f"yg4_{t}") for t in range(4)]
        yg = yg4[:, 0, :]
        ygB = yg4B[0]
        pend_tr = {}
        pend_att = {}
        ss2 = k.sb("ss2", [128, 1], F32)
        ss2B = Buf("ss2")
        rstd2 = k.sb("rstd2", [128, 1], F32)
        rstd2B = Buf("rstd2")
        HALF = NFC // 2
        gact = k.sb("gact", [128, HALF, 512], BF16)
        gactB = [Buf(f"gact{j}") for j in range(HALF)]
        ext = [k.sb(f"ext{i}", [128, 516], F32) for i in range(2)]
        extB = [Buf(f"ext{i}") for i in range(2)]
        ycv = [gtmp, k.sb("ycv1", [128, 512], F32)]
        ycvB = [gtmpB, Buf("ycv1")]
        tail = k.sb("tail", [128, 88, 2], F32)
        tailB = Buf("tail")
        MS(DVE, tail[:], 0.0, [tailB])
        pin = otmp
        pinB = otmpB
        pinT = k.dma_track("pin")
        pbf = yg
        pbfB = ygB
        pT = k.sb("pT", [128, 2, 512], BF16)
        pTB = Buf("pT")
        gsig = ycv[1]
        gsigB = ycvB[1]
        fnw = k.sb("fnw", [128, D], F32)
        fnwB = Buf("fnw")
        k.dma(SP, tconst, fnw[:], fnw_d, writes=[fnwB])
        oT = [k.dma_track(f"out{t}") for t in range(4)]

        def attn_kv(p, wv_, wvB, nt, kdst, kdstB, vdst_fn, vdstB):
            N = nt * 128
            bk, bB = nf()
            for c in range(16):
                MM(bk[:, 0:N], wv_[:, c, 128:256], hT[:, c, 0:N], c == 0, c == 15, [wvB, hTB[c]], [bB])
            CP(ACT_, kdst[:, 0:N], bk[:, 0:N], [bB], [kdstB])
            bk, bB = nf()
            for t in range(nt):
                for c in range(16):
                    MM(bk[:, t * 128:(t + 1) * 128], hT[:, c, t * 128:(t + 1) * 128], wv_[:, c, 256:384], c == 0, c == 15,
                       [wvB, hTB[c]], [bB])
            for t in range(nt):
                CP(DVE, vdst_fn(t), bk[:, t * 128:(t + 1) * 128].rearrange("p (h d) -> p h d", h=2), [bB], [vdstB])

        def attention_pair(p, wv_, wvB, nt, tile0):
            N = nt * 128
            bk, bB = nf()
            for c in range(16):
                MM(bk[:, 0:N], wv_[:, c, 0:128], hT[:, c, 0:N], c == 0, c == 15, [wvB, hTB[c]], [bB])
            A(qT[:, 0:N], bk[:, 0:N], AF.Copy, [bB], [qTB], scale=0.125)
            attn_kv(p, wv_, wvB, nt, kTcur, kTcurB, lambda t: Vcur[:, t, :, 0:64], VcurB)
            if "f" in pend_att:
                pend_att.pop("f")()
            yp, ypB_ = (ypair, ypairB) if p % 2 == 0 else (ktem, ktemB)
            obs = [(pf[4], pfB[4]), (pf[5], pfB[5])]
            for hh in range(2):
                k.dma(SP, biasTr[hh], biasb[hh][:], biasT_d[2 * p + hh], writes=[biasB[hh]])
            steps = []
            for j in range(4 + nt):
                i0 = max(j - 4, 0)
                i1 = min(j, nt - 1)
                if i1 < i0:
                    continue
                for hh in range(2):
                    steps.append((hh, j, i0, i1))

            def src_of(hh, j):
                ps_ = slice(64 * hh, 64 * hh + 64)
                if j < 4:
                    return (kTprev[ps_, p, j * 128:(j + 1) * 128], kTprevB[p], Vprev[:, j, 2 * p + hh, :], VprevB[p])
                return (kTcur[ps_, (j - 4) * 128:(j - 3) * 128], kTcurB, Vcur[:, j - 4, hh, :], VcurB)

            def stageA(st_):
                hh, j, i0, i1 = st_
                ps_ = slice(64 * hh, 64 * hh + 64)
                ncol = (i1 - i0 + 1) * 128
                ksrc, ksB, _, _ = src_of(hh, j)
                sb_, sB = nf()
                MM(sb_[:, 0:ncol], ksrc, qT[ps_, i0 * 128:(i1 + 1) * 128], True, True, [ksB, qTB], [sB])
                pi = cnt["pt"] % NPT
                cnt["pt"] += 1
                b0 = (i0 - j + 4) * 128
                STT(DVE, sb_[:, 0:ncol], sb_[:, 0:ncol], 70.0, biasb[hh][:, b0:b0 + ncol], ALU.min, ALU.add,
                    [sB, biasB[hh]], [sB])
                u = tile0 + j
                A(PT[pi][:, 0:ncol], sb_[:, 0:ncol], AF.Exp, [sB, constB], [PTB[pi]], bias=keymask[:, u:u + 1])
                return pi

            def stageB(st_, pi):
                hh, j, i0, i1 = st_
                _, _, vsrc, vsB = src_of(hh, j)
                ob, oB = obs[hh]
                for i in range(i0, i1 + 1):
                    MM(ob[:, i * 65:(i + 1) * 65], PT[pi][:, (i - i0) * 128:(i - i0 + 1) * 128], vsrc,
                       j == 0 and i == 0, j == 3 + nt and i == nt - 1, [PTB[pi], vsB], [oB])
            pend = []
            for st_ in steps:
                pend.append((st_, stageA(st_)))
                if len(pend) > 3:
                    s0, p0 = pend.pop(0)
                    stageB(s0, p0)
            for s0, p0 in pend:
                stageB(s0, p0)
            for hh in range(2):
                ob, oB = obs[hh]
                ov = ob[:, 0:nt * 65].rearrange("p (t e) -> p t e", e=65)
                TS(DVE, rec[:, hh, 0:nt], ov[:, :, 64], 1e-30, None, ALU.max, None, [oB], [recB[hh]])
                k.op(DVE, lambda hh=hh: nc.vector.reciprocal(out=rec[:, hh, 0:nt], in_=rec[:, hh, 0:nt]), [recB[hh]], [recB[hh]])
                for i in range(nt):
                    if hh == 0:
                        A(yp[:, i, 0:64], ob[:, i * 65:i * 65 + 64], AF.Copy, [oB, recB[hh]], [ypB_],
                          scale=rec[:, hh, i:i + 1])
                    else:
                        TS(DVE, yp[:, i, 64:128], ob[:, i * 65:i * 65 + 64], rec[:, hh, i:i + 1], None,
                           ALU.mult, None, [oB, recB[hh]], [ypB_])
            def finish(p=p, nt=nt, N=N, yp=yp, ypB_=ypB_):
                bk, bB = nb()
                for i in range(nt):
                    TR(bk[:, i * 128:(i + 1) * 128], yp[:, i, :], [ypB_, constB], [bB])
                CP(ACT_, yT[:, p, 0:N], bk[:, 0:N], [bB], [yTB[p]])
            if p < 7:
                pend_att["f"] = finish
            else:
                finish()
            shift_prev(p, nt)

        def shift_prev(p, nt):
            for jj in range(4 - nt):
                CP(POOL, kTprev[:, p, jj * 128:(jj + 1) * 128], kTprev[:, p, (jj + nt) * 128:(jj + nt + 1) * 128],
                   [kTprevB[p]], [kTprevB[p]])
                CP(POOL, Vprev[:, jj, 2 * p:2 * p + 2, :], Vprev[:, jj + nt, 2 * p:2 * p + 2, :], [VprevB[p]], [VprevB[p]])
            CP(POOL, kTprev[:, p, (4 - nt) * 128:512], kTcur[:, 0:nt * 128], [kTcurB], [kTprevB[p]])
            CP(POOL, Vprev[:, 4 - nt:4, 2 * p:2 * p + 2, :], Vcur[:, 0:nt, :, :], [VcurB], [VprevB[p]])

        def gla_head_a(h, wa, waB, nt):
            N = nt * 128
            hs = slice(h * 128, (h + 1) * 128)
            bkb, bbB = nf()
            bke, beB = nf()
            for t in range(nt):
                MM(bkb[:, t * 128:(t + 1) * 128], la[:, t, hs], triu[:, :], True, True, [laB[t], constB], [bbB])
                MM(bke[:, t * 128:(t + 1) * 128], tril[:, :], la[:, t, hs], True, True, [laB[t], constB], [beB])
            A(eb[:, 0:N], bkb[:, 0:N], AF.Exp, [bbB], [ebB])
            A(enb[:, 0:N], bkb[:, 0:N], AF.Exp, [bbB], [enbB], scale=-1.0)
            A(ee2m[:, 0:N], bke[:, 0:N], AF.Exp, [beB], [ee2mB])
            bk, bB = nf()
            for c in range(16):
                MM(bk[:, 0:N], wa[:, c, 0:128], hT[:, c, 0:N], c == 0, c == 15, [waB, hTB[c]], [bB])
            STT(DVE, qdec[:, 0:N], bk[:, 0:N], 128.0 ** -0.5, eb[:, 0:N], ALU.mult, ALU.mult, [bB, ebB], [qdecB])
            bk, bB = nf()
            for c in range(16):
                MM(bk[:, 0:N], wa[:, c, 128:256], hT[:, c, 0:N], c == 0, c == 15, [waB, hTB[c]], [bB])
            TT(DVE, kinv[:, 0:N], bk[:, 0:N], enb[:, 0:N], ALU.mult, [bB, enbB], [kinvB])
            bk, bB = nf()
            for t in range(nt):
                for c in range(16):
                    MM(bk[:, t * 128:(t + 1) * 128], hT[:, c, t * 128:(t + 1) * 128], wa[:, c, 128:256], c == 0, c == 15,
                       [waB, hTB[c]], [bB])
            TT(DVE, ktem[:, 0:nt, :], bk[:, 0:N].rearrange("p (t d) -> p t d", d=128),
               ee2m[:, 0:N].rearrange("p (t d) -> p t d", d=128), ALU.mult, [bB, ee2mB], [ktemB])
            if "f" in pend_tr:
                pend_tr.pop("f")()

        def gla_head_b(h, wb_, wbB, nt):
            N = nt * 128
            ba, baB = nf()
            for t in range(nt):
                tsl = slice(t * 128, (t + 1) * 128)
                MM(ba[:, tsl], kinv[:, tsl], qdec[:, tsl], True, True, [kinvB, qdecB], [baB])
            for t in range(nt):
                TT(DVE, attT4[:, t, :], ba[:, t * 128:(t + 1) * 128], maskT[:], ALU.mult, [baB, constB], [attT4B[t]])
            for t0 in range(0, nt, 2):
                bk, bB = nf()
                tn = min(2, nt - t0)
                for t in range(t0, t0 + tn):
                    for c in range(16):
                        MM(bk[:, (t - t0) * 256:(t - t0 + 1) * 256], hT[:, c, t * 128:(t + 1) * 128], wb_[:, c, 0:256],
                           c == 0, c == 15, [wbB, hTB[c]], [bB])
                CP(ACT_, vg[:, t0:t0 + tn, :], bk[:, 0:tn * 256].rearrange("p (t d) -> p t d", d=256), [bB], [vgB])
            bus = []
            for t0 in range(0, nt, 2):
                bu, buB = nf()
                for t in range(t0, min(t0 + 2, nt)):
                    MM(bu[:, (t - t0) * 256:(t - t0 + 1) * 256], ktem[:, t, :], vg[:, t, :], True, True, [ktemB, vgB], [buB])
                bus.append((bu, buB))
            for t in range(nt):
                CP(ACT_, Sbf4[:, t, :], S[:, h, :], [SB_], [Sbf4B[t]])
                bu, buB = bus[t // 2]
                STT(DVE, S[:, h, :], S[:, h, :], eb[:, t * 128 + 127:t * 128 + 128], bu[:, (t % 2) * 256:(t % 2 + 1) * 256],
                    ALU.mult, ALU.add, [SB_, ebB, buB], [SB_])
            for t0 in range(0, nt, 2):
                bk, bB = nf()
                tn = min(2, nt - t0)
                for t in range(t0, t0 + tn):
                    for c in range(16):
                        MM(bk[:, (t - t0) * 256:(t - t0 + 1) * 256], hT[:, c, t * 128:(t + 1) * 128], wb_[:, c, 256:512],
                           c == 0, c == 15, [wbB, hTB[c]], [bB])
                A(silr[:, t0:t0 + tn, :], bk[:, 0:tn * 256].rearrange("p (t d) -> p t d", d=256), AF.Silu, [bB], [silrB])
            bos = []
            for t0 in range(0, nt, 2):
                bo, boB = nf()
                for t in range(t0, min(t0 + 2, nt)):
                    tsl = slice(t * 128, (t + 1) * 128)
                    osl = slice((t - t0) * 256, (t - t0 + 1) * 256)
                    MM(bo[:, osl], attT4[:, t, :], vg[:, t, :], True, False, [attT4B[t], vgB], [boB])
                    MM(bo[:, osl], qdec[:, tsl], Sbf4[:, t, :], False, True, [qdecB, Sbf4B[t]], [boB])
                bos.append((bo, boB))
            for t in range(nt):
                bo, boB = bos[t // 2]
                osl = slice((t % 2) * 256, (t % 2 + 1) * 256)
                MS(DVE, ss2[:], 0.0, [ss2B])
                A(yg4[:, t, :], bo[:, osl], AF.Square, [boB], [yg4B[t], ss2B], scale=1.0 / 16.0, accum_out=ss2[:])
                RSTD(rstd2, rstd2B, ss2, ss2B)
                STT(DVE, otmp[:], bo[:, osl], rstd2[:, 0:1], gnw[:], ALU.mult, ALU.mult, [boB, rstd2B, constB], [otmpB])
                TT(DVE, yg4[:, t, :], otmp[:], silr[:, t, :], ALU.mult, [otmpB, silrB], [yg4B[t]])

            def finish_tr(h=h, nt=nt, N=N):
                ytb, ytB = nb()
                for t in range(nt):
                    for hf in range(2):
                        TR(ytb[:, hf * 512 + t * 128:hf * 512 + (t + 1) * 128], yg4[:, t, hf * 128:(hf + 1) * 128],
                           [yg4B[t], constB], [ytB])
                for hf in range(2):
                    CP(ACT_, yT[:, 8 + 2 * h + hf, 0:N], ytb[:, hf * 512:hf * 512 + N], [ytB], [yTB[8 + 2 * h + hf]])
            if h < 3:
                pend_tr["f"] = finish_tr
            else:
                finish_tr()

        widx = [i for i, it in enumerate(items) if it[0] is not None]
        loaded = {}
        nxt = 0

        def prefetch(upto):
            nonlocal nxt
            while nxt < len(widx) and nxt <= upto:
                i = widx[nxt]
                loaded[i] = load_w(items[i][0], items[i][2], items[i][3])
                nxt += 1
        wpos = 0
        for i, (parts, fn, key, first) in enumerate(items):
            if max_items is not None and i >= max_items:
                break
            if parts is None:
                fn()
            else:
                prefetch(wpos + NW - 1)
                v, vB, bi, n = loaded.pop(i)
                fn(v, vB)
                wpos += 1

        fin = Buf("fin")
        fin.r = {tr_: tr_.count for tr_ in oT}
        for tr in dbg_outs.values():
            fin.r[tr] = tr.count
        k._deps(SP, [], [fin])
        build.stats = {e.name: (e.nops, e.nwaits) for e in k.engs}
    return nc


def _host_inputs(inputs):
    f = np.float32
    x = np.asarray(inputs["x"], f).reshape(T, D)
    p = np.asarray(inputs["p"], f).reshape(T, 256)
    table = np.asarray(inputs["att_rel_bias"], f).reshape(16, 513)
    kk = np.arange(128)[:, None]
    qq = np.arange(640)[None, :]
    dist = np.clip(qq - kk, -256, 256) + 256
    kc = kk // 64
    qc = qq // 64
    valid = (qc - kc >= 0) & (qc - kc <= 8)
    biasT = np.where(valid[None], table[:, dist], f(NEG)).astype(f)
    ident = np.eye(128, dtype=f)
    ii = np.arange(128)
    maskT = (ii[None, :] >= ii[:, None]).astype(f)
    triu = np.where(ii[:, None] <= ii[None, :], f(-1.0 / 16.0), f(0.0)).astype(f)
    tril = np.where(ii[:, None] > ii[None, :], f(-1.0 / 16.0), f(0.0)).astype(f)

    def col16(v):
        return np.ascontiguousarray(np.asarray(v, f).reshape(16, 128).T)

    convw = np.ascontiguousarray(np.asarray(inputs["w_ffn_conv"], f).reshape(3, 88, 128).transpose(2, 0, 1))
    convb = np.ascontiguousarray(np.asarray(inputs["b_ffn_conv"], f).reshape(88, 128).T)
    gnw = np.ascontiguousarray(np.broadcast_to(np.asarray(inputs["gla_norm_w"], f).reshape(1, 256), (128, 256)))
    fnw = np.ascontiguousarray(np.broadcast_to(np.asarray(inputs["final_norm_w"], f).reshape(1, D), (128, D)))
    wup = np.zeros((32, 512), f)
    wup[0:16] = np.asarray(inputs["w_gla_gate_up"], f).reshape(16, 512)
    wup[16] = np.asarray(inputs["b_gla_gate"], f).reshape(512)
    shared = {
        "biasT": biasT, "ident": ident, "maskT": maskT, "triu": triu, "tril": tril,
        "nmix": col16(inputs["norm_mix_w"]), "nffn": col16(inputs["norm_ffn_w"]), "nple": col16(inputs["norm_ple_w"]),
        "convw": convw, "convb": convb, "gnw": gnw, "fnw": fnw, "wupaug": wup,
        "w_in": np.ascontiguousarray(np.asarray(inputs["w_in"], f).reshape(D, 6160)),
        "w_out": np.ascontiguousarray(np.asarray(inputs["w_out"], f).reshape(D, D)),
        "w_ffn_up": np.ascontiguousarray(np.asarray(inputs["w_ffn_up"], f).reshape(D, 2 * DFF)),
        "w_ffn_down": np.ascontiguousarray(np.asarray(inputs["w_ffn_down"], f).reshape(DFF, D)),
        "w_ple_gate": np.ascontiguousarray(np.asarray(inputs["w_ple_gate"], f).reshape(D, D)),
        "w_ple_proj": np.ascontiguousarray(np.asarray(inputs["w_ple_proj"], f).reshape(256, D)),
    }
    in_maps = []
    for c in range(NCORE):
        m0 = TOK * c - 128
        start = m0 - PREF
        xcc = np.zeros((XC, D), f)
        lo = max(0, -start)
        xcc[lo:] = x[start + lo:start + XC]
        pcc = np.zeros((MAIN, 256), f)
        lo2 = max(0, -m0)
        pcc[lo2:] = p[m0 + lo2:m0 + MAIN]
        km = np.zeros((128, 4 + MAIN_TILES), f)
        tok = (m0 - 512) + 128 * np.arange(4 + MAIN_TILES)[None, :] + np.arange(128)[:, None]
        km[tok < 0] = NEG
        d = dict(shared)
        d["xc"] = xcc
        d["pc"] = pcc
        d["keymask"] = km
        in_maps.append(d)
    return in_maps


_NC = None


def kernel(**inputs):
    global _NC
    in_maps = _host_inputs(inputs)
    if _NC is None:
        _NC = build()
    res = run_bass_kernel_spmd(_NC, in_maps, core_ids=list(range(NCORE)))
    outs = [np.asarray(res.results[c]["out"], np.float32).reshape(TOK, D) for c in range(NCORE)]
    return np.concatenate(outs, axis=0).reshape(1, T, D)
```
